# Optimizing a Trainium2 kernel written in Bass

```python
import jax, jax.numpy as jnp
from jax import lax
import numpy as np

D_MODEL = 1024
BATCH = 8
SEQ = 8192
DEPTH = 1

CHUNK = 128
MLP_GROUPS = 8
MLP_GROUP_DIM = 128
MLP_WIDTH = MLP_GROUPS * MLP_GROUP_DIM
WINDOW = 128
N_Q_HEADS = 16
N_KV_HEADS = 2
HEAD_DIM = 64
Q_PER_KV = N_Q_HEADS // N_KV_HEADS
ATTN_WIDTH = N_Q_HEADS * HEAD_DIM
KV_WIDTH = N_KV_HEADS * HEAD_DIM
N_BRANCHES = 2
SPLITS = (2 * MLP_WIDTH,
          2 * MLP_WIDTH + ATTN_WIDTH,
          2 * MLP_WIDTH + ATTN_WIDTH + KV_WIDTH,
          2 * MLP_WIDTH + ATTN_WIDTH + 2 * KV_WIDTH)
IN_COLS = 2 * MLP_WIDTH + ATTN_WIDTH + 2 * KV_WIDTH + N_BRANCHES * D_MODEL
PEER_HEADS = 8
PEER_NKEYS = 128
PEER_EXPERTS = PEER_NKEYS * PEER_NKEYS
PEER_TOPK = 16
PEER_QDIM = 256
PEER_HALF = PEER_QDIM // 2
PEER_BLOCK = 128
ALPHA = (2.0 * DEPTH) ** 0.25
BETA = (8.0 * DEPTH) ** -0.25
LN_EPS = 1e-5
N_MOD = 6

kernel_name = 'hybrid_gmlp_swa_peer_block'


def layer_norm(x, g, b):
    xf = x.astype(jnp.float32)
    mu = jnp.mean(xf, axis=-1, keepdims=True)
    var = jnp.mean(jnp.square(xf - mu), axis=-1, keepdims=True)
    y = (xf - mu) * lax.rsqrt(var + LN_EPS)
    return (y * g.astype(jnp.float32) + b.astype(jnp.float32)).astype(x.dtype)


def gmlp_spatial_gating(z_uv, lnv_g, lnv_b, w_s, b_s):
    B, S, _ = z_uv.shape
    z = jax.nn.gelu(z_uv, approximate=False)
    u, v = jnp.split(z, 2, axis=-1)
    v = layer_norm(v, lnv_g, lnv_b)
    v = v.reshape(B, S // CHUNK, CHUNK, MLP_GROUPS, MLP_GROUP_DIM)
    causal = jnp.tril(jnp.ones((CHUNK, CHUNK), dtype=bool))
    w = jnp.where(causal[None], w_s, jnp.zeros_like(w_s))
    mixed = jnp.einsum('gts,bcsgd->bctgd', w, v) + b_s.T[:, :, None]
    return u * mixed.reshape(B, S, MLP_WIDTH)


def sliding_window_attention(q, k, v, sinks):
    B, S, _ = q.shape
    nb = S // WINDOW
    q = q.reshape(B, nb, WINDOW, N_KV_HEADS, Q_PER_KV, HEAD_DIM)
    k = k.reshape(B, nb, WINDOW, N_KV_HEADS, HEAD_DIM)
    v = v.reshape(B, nb, WINDOW, N_KV_HEADS, HEAD_DIM)

    def with_prev(t):
        prev = jnp.pad(t, ((0, 0), (1, 0), (0, 0), (0, 0), (0, 0)))[:, :-1]
        return jnp.concatenate([prev, t], axis=2)

    kb, vb = with_prev(k), with_prev(v)
    scores = jnp.einsum('bnqhgd,bnkhd->bnhgqk', q, kb).astype(jnp.float32) * (HEAD_DIM ** -0.5)
    qi = jnp.arange(WINDOW)[:, None] + WINDOW
    ki = jnp.arange(2 * WINDOW)[None, :]
    band = (ki <= qi) & (ki > qi - WINDOW)
    has_prev = jnp.arange(nb)[:, None, None] > 0
    valid = band[None] & (has_prev | (ki[None] >= WINDOW))
    scores = jnp.where(valid[None, :, None, None], scores, jnp.finfo(jnp.float32).min)
    sink = sinks.astype(jnp.float32).reshape(N_KV_HEADS, Q_PER_KV)[None, None, :, :, None, None]
    sink = jnp.broadcast_to(sink, scores.shape[:-1] + (1,))
    probs = jax.nn.softmax(jnp.concatenate([scores, sink], axis=-1), axis=-1)[..., :-1]
    out = jnp.einsum('bnhgqk,bnkhd->bnqhgd', probs.astype(vb.dtype), vb)
    return out.reshape(B, S, ATTN_WIDTH)


def peer_layer(xt, w_pq, sub_k1, sub_k2, peer_u, peer_v):
    T, D = xt.shape

    def block(xb):
        q = (xb @ w_pq).reshape(-1, PEER_HEADS, PEER_QDIM)
        q1, q2 = q[..., :PEER_HALF], q[..., PEER_HALF:]
        s1 = jnp.einsum('thd,hkd->thk', q1, sub_k1).astype(jnp.float32)
        s2 = jnp.einsum('thd,hkd->thk', q2, sub_k2).astype(jnp.float32)
        v1, i1 = lax.top_k(s1, PEER_TOPK)
        v2, i2 = lax.top_k(s2, PEER_TOPK)
        cand = (v1[..., :, None] + v2[..., None, :]).reshape(-1, PEER_HEADS, PEER_TOPK * PEER_TOPK)
        sc, ci = lax.top_k(cand, PEER_TOPK)
        e = (jnp.take_along_axis(i1, ci // PEER_TOPK, axis=-1) * PEER_NKEYS
             + jnp.take_along_axis(i2, ci % PEER_TOPK, axis=-1))
        g = jax.nn.softmax(sc, axis=-1)
        act = jax.nn.gelu(jnp.einsum('thkd,td->thk', peer_u[e], xb).astype(jnp.float32),
                          approximate=False)
        return jnp.einsum('thk,thkd->td', (g * act).astype(xb.dtype), peer_v[e])

    out = lax.map(block, xt.reshape(T // PEER_BLOCK, PEER_BLOCK, D))
    return out.reshape(T, D)


def setup_inputs(seed: int = 0) -> dict:
    key = jax.random.key(seed)
    ks = jax.random.split(key, 24)
    nrm = lambda k, shape, s: jax.random.normal(k, shape, jnp.float32) * s
    L, D = DEPTH, D_MODEL
    return {
        'x': nrm(ks[0], (BATCH, SEQ, D), 1.0),
        'c': nrm(ks[1], (BATCH, D), 1.0),
        'w_ada': nrm(ks[2], (L, D, N_MOD * D), 0.1 * D ** -0.5),
        'b_ada': nrm(ks[3], (L, N_MOD * D), 0.02),
        'w_in': nrm(ks[4], (L, D, IN_COLS), D ** -0.5),
        'lnv_g': 1.0 + nrm(ks[5], (L, MLP_WIDTH), 0.02),
        'lnv_b': nrm(ks[6], (L, MLP_WIDTH), 0.02),
        'w_spatial': nrm(ks[7], (L, MLP_GROUPS, CHUNK, CHUNK), CHUNK ** -0.5),
        'b_spatial': 1.0 + nrm(ks[8], (L, MLP_GROUPS, CHUNK), 0.1),
        'attn_sinks': nrm(ks[9], (L, N_Q_HEADS), 0.5),
        'w_proj_a': nrm(ks[10], (L, MLP_WIDTH, D), BETA * MLP_WIDTH ** -0.5),
        'w_proj_b': nrm(ks[11], (L, ATTN_WIDTH, D), BETA * ATTN_WIDTH ** -0.5),
        'w_out': nrm(ks[12], (L, D, D), BETA * D ** -0.5),
        'ln1_g': 1.0 + nrm(ks[13], (L, D), 0.02),
        'ln1_b': nrm(ks[14], (L, D), 0.02),
        'w_pq': nrm(ks[15], (L, D, PEER_HEADS * PEER_QDIM), D ** -0.5),
        'sub_keys1': nrm(ks[16], (L, PEER_HEADS, PEER_NKEYS, PEER_HALF), PEER_HALF ** -0.5),
        'sub_keys2': nrm(ks[17], (L, PEER_HEADS, PEER_NKEYS, PEER_HALF), PEER_HALF ** -0.5),
        'peer_u': nrm(ks[18], (L, PEER_EXPERTS, D), D ** -0.5),
        'peer_v': nrm(ks[19], (L, PEER_EXPERTS, D), BETA * PEER_HEADS ** -0.5),
        'ln2_g': 1.0 + nrm(ks[20], (L, D), 0.02),
        'ln2_b': nrm(ks[21], (L, D), 0.02),
    }


def reference(x, c, w_ada, b_ada, w_in, lnv_g, lnv_b, w_spatial, b_spatial, attn_sinks,
              w_proj_a, w_proj_b, w_out, ln1_g, ln1_b, w_pq, sub_keys1, sub_keys2,
              peer_u, peer_v, ln2_g, ln2_b):
    B, S, D = x.shape
    for l in range(DEPTH):
        mod = jax.nn.silu(c) @ w_ada[l] + b_ada[l]
        sh1, sc1, gt1, sh2, sc2, gt2 = [m[:, None, :] for m in jnp.split(mod, N_MOD, axis=-1)]

        h = x * (1.0 + sc1) + sh1
        proj = h @ w_in[l]
        z_uv, q, k, v, gates = jnp.split(proj, SPLITS, axis=-1)
        a = gmlp_spatial_gating(z_uv, lnv_g[l], lnv_b[l], w_spatial[l], b_spatial[l])
        o = sliding_window_attention(q, k, v, attn_sinks[l])
        g_a, g_b = jnp.split(jax.nn.sigmoid(gates), N_BRANCHES, axis=-1)
        merged = g_a * (a @ w_proj_a[l]) + g_b * (o @ w_proj_b[l])
        mix = merged @ w_out[l]
        x = layer_norm(ALPHA * x + (1.0 + gt1) * mix, ln1_g[l], ln1_b[l])

        h = x * (1.0 + sc2) + sh2
        f = peer_layer(h.reshape(B * S, D), w_pq[l], sub_keys1[l], sub_keys2[l],
                       peer_u[l], peer_v[l]).reshape(B, S, D)
        x = layer_norm(ALPHA * x + (1.0 + gt2) * f, ln2_g[l], ln2_b[l])
    return x
```

```python
import numpy as np
from contextlib import ExitStack
import concourse.bass as bass
import concourse.mybir as mybir
from concourse.bass_utils import run_bass_kernel_spmd

F32 = mybir.dt.float32
BF16 = mybir.dt.bfloat16
I32 = mybir.dt.int32
U32 = mybir.dt.uint32
AF = mybir.ActivationFunctionType
ALU = mybir.AluOpType
AX = mybir.AxisListType

D = 1024
SEQ = 8192
NCORES = 8
ALPHA = 2.0 ** 0.25
LN_EPS = 1e-5
NEXP = 16384
NEG = -30000.0


class Tracker:
    def __init__(self, nc):
        self.nc = nc
        self.eng = dict(pe=nc.tensor, act=nc.scalar, dve=nc.vector, pool=nc.gpsimd, sp=nc.sync)
        self.agents = {}
        for e in ("pe", "act", "dve", "pool"):
            self.agents[e] = [nc.alloc_semaphore("s_" + e), 0, 1]
        self.seen = {e: {} for e in self.eng}
        self.bufs = {}

    def _agent(self, name):
        if name not in self.agents:
            self.agents[name] = [self.nc.alloc_semaphore("c_" + name), 0, 16]
        return self.agents[name]

    def op(self, issuer, fn, reads=(), writes=(), chan=None):
        agent = chan if chan is not None else issuer
        ag = self._agent(agent)
        need = {}

        def add(a, t):
            if need.get(a, 0) < t:
                need[a] = t

        for b in reads:
            st = self.bufs.get(b)
            if st is not None and st[0] is not None:
                add(*st[0])
        for b in writes:
            st = self.bufs.get(b)
            if st is not None:
                if st[0] is not None and (st[0][0] != agent or chan is not None):
                    add(*st[0])
                for a, t in st[1].items():
                    if a != agent or chan is not None:
                        add(a, t)
        eng = self.eng[issuer]
        seen = self.seen[issuer]
        for a, t in need.items():
            if seen.get(a, 0) < t:
                sem, _, step = self.agents[a]
                eng.wait_ge(sem, t * step)
                seen[a] = t
        ins = fn()
        ag[1] += 1
        tick = ag[1]
        ins.then_inc(ag[0], ag[2])
        for b in reads:
            st = self.bufs.setdefault(b, [None, {}])
            st[1][agent] = tick
        for b in writes:
            self.bufs[b] = [(agent, tick), {}]
        return tick

    def barrier(self):
        for issuer, eng in self.eng.items():
            seen = self.seen[issuer]
            for a, (sem, cnt, step) in self.agents.items():
                if cnt > 0 and seen.get(a, 0) < cnt:
                    eng.wait_ge(sem, cnt * step)
                    seen[a] = cnt


def build(NT=SEQ // 128, phases='both', NBU=12, POOL_DIAG_EVERY=0, TPS=2):
    S = NT * 128
    nc = bass.Bass("TRN2", target_bir_lowering=False)
    T = Tracker(nc)

    def din(name, shape, dt=F32):
        return nc.dram_tensor(name, list(shape), dt, kind="ExternalInput").ap()

    x_d = din("x", [S, D])
    ct_d = din("c_t", [128, 8])
    wada_d = din("w_ada", [D, 6 * D])
    badaT_d = din("b_ada_t", [128, 48])
    win_d = din("w_in", [D, 5376])
    wkd_d = din("w_kdup", [D, 256])
    lnvg_d = din("lnv_g", [D])
    lnvb_d = din("lnv_b", [D])
    wsp_d = din("wsp_t", [128, 8, 128])
    bsp_d = din("b_sp", [D])
    sink_d = din("sinks", [16])
    wpa_d = din("w_proj_a", [D, D])
    wpb_d = din("w_proj_b", [D, D])
    wout_d = din("w_out", [D, D])
    ln1g_d = din("ln1_g", [D])
    ln1b_d = din("ln1_b", [D])
    wpq_d = din("w_pq", [D, 2048])
    keys_d = din("keys_t", [128, 16, 128])
    pu_d = din("peer_u", [NEXP, D])
    pv_d = din("peer_v", [NEXP, D])
    ln2g_d = din("ln2_g", [D])
    ln2b_d = din("ln2_b", [D])
    out_d = nc.dram_tensor("out", [S, D], F32, kind="ExternalOutput").ap()
    x1_d = nc.dram_tensor("x1_scratch", [S, D], F32, kind="Internal").ap()
    tab_d = nc.dram_tensor("uv_table", [NEXP, 2 * D], BF16, kind="Internal").ap()

    V, A, P, PE, SP = nc.vector, nc.scalar, nc.gpsimd, nc.tensor, nc.sync

    def sb(name, shape, dt):
        return nc.alloc_sbuf_tensor(name, list(shape), dt)

    PS = [nc.alloc_psum_tensor("ps%d" % i, [128, 1024], F32) for i in range(4)]
    ps_rr = [0]

    ps_n = [4]

    def nextps(n=None):
        n = n or ps_n[0]
        i = ps_rr[0] % n
        ps_rr[0] += 1
        return PS[i], "ps%d" % i

    identf = sb("identf", [128, 128], F32)
    identb = sb("identb", [128, 128], BF16)
    onesf = sb("onesf", [128, 128], F32)
    modT = sb("modT", [128, 48], F32)
    stat = sb("stat", [128, 64], F32)
    bc_tmp = sb("bc_tmp", [128, 128], F32)
    epsc = sb("epsc", [128, 1], F32)

    T.op("pool", lambda: P.memset(onesf[:], 1.0), writes=["onesf"])
    T.op("pool", lambda: P.affine_select(out=identf[:], in_=onesf[:], pattern=[[1, 128]], compare_op=ALU.is_equal,
                                         fill=0.0, base=0, channel_multiplier=-1), reads=["onesf"], writes=["identf"])
    T.op("dve", lambda: V.tensor_copy(out=identb[:], in_=identf[:]), reads=["identf"], writes=["identb"])

    def bcast_row(dst, dst_name, src_d, chan):
        T.op("sp", lambda: SP.dma_start(out=dst, in_=src_d.partition_broadcast(128)), writes=[dst_name], chan=chan)

    esw = ExitStack()
    win = esw.enter_context(nc.sbuf_tensor("win", [128, 8, 5376], BF16))
    wkd = esw.enter_context(nc.sbuf_tensor("wkd", [128, 8, 256], BF16))
    wpa = esw.enter_context(nc.sbuf_tensor("wpa", [128, 8, 1024], BF16))
    wpb = esw.enter_context(nc.sbuf_tensor("wpb", [128, 8, 1024], BF16))
    wout = esw.enter_context(nc.sbuf_tensor("wout", [128, 8, 1024], BF16))

    def load_w(dst, dname, src, ncols, step=1024):
        for k in range(8):
            for n0 in range(0, ncols, step):
                n1 = min(ncols, n0 + step)
                T.op("pool", lambda k=k, n0=n0, n1=n1: P.dma_start(out=dst[:, k, n0:n1], in_=src[k * 128:(k + 1) * 128, n0:n1]),
                     writes=[dname], chan="ldw_" + dname)
    if phases != 'p2':
        load_w(win, "win", win_d, 5376, step=1792)
        load_w(wkd, "wkd", wkd_d, 256)
        load_w(wpa, "wpa", wpa_d, 1024)
        load_w(wpb, "wpb", wpb_d, 1024)
        load_w(wout, "wout", wout_d, 1024)

    with ExitStack() as es:
        ct = es.enter_context(nc.sbuf_tensor("ct", [128, 8], F32))
        silc = es.enter_context(nc.sbuf_tensor("silc", [128, 8], F32))
        badaT = es.enter_context(nc.sbuf_tensor("badaT", [128, 48], F32))
        wa0 = es.enter_context(nc.sbuf_tensor("wa0", [128, 8, 256], F32))
        wa1 = es.enter_context(nc.sbuf_tensor("wa1", [128, 8, 256], F32))
        T.op("sp", lambda: SP.dma_start(out=ct[:], in_=ct_d), writes=["ct"], chan="ld_ct")
        T.op("sp", lambda: SP.dma_start(out=badaT[:], in_=badaT_d), writes=["badaT"], chan="ld_bada")
        T.op("act", lambda: A.activation(out=silc[:], in_=ct[:], func=AF.Silu), reads=["ct"], writes=["silc"])
        was = [wa0, wa1]
        pm, pmn = PS[3], "ps3"
        for nb in range(24):
            w = was[nb % 2]
            wn = "wa%d" % (nb % 2)
            T.op("sp", lambda w=w, nb=nb: SP.dma_start(
                out=w[:], in_=wada_d[:, nb * 256:(nb + 1) * 256].rearrange("(k p) n -> p k n", p=128)),
                writes=[wn], chan="ld_" + wn)

            def f(w=w, nb=nb):
                ins = None
                for jj in range(2):
                    j = nb * 2 + jj
                    for k in range(8):
                        ins = PE.matmul(pm[:, j:j + 1], lhsT=w[:, k, jj * 128:(jj + 1) * 128], rhs=silc[:, k:k + 1],
                                        start=(k == 0), stop=(k == 7))
                return ins
            T.op("pe", f, reads=[wn, "silc"], writes=[pmn])
        T.op("dve", lambda: V.tensor_tensor(out=modT[:], in0=pm[:, 0:48], in1=badaT[:], op=ALU.add),
             reads=[pmn, "badaT"], writes=["modT"])
        T.op("dve", lambda: V.tensor_scalar_add(out=modT[:, 8:24], in0=modT[:, 8:24], scalar1=1.0),
             reads=["modT"], writes=["modT"])
        T.op("dve", lambda: V.tensor_scalar_add(out=modT[:, 32:48], in0=modT[:, 32:48], scalar1=1.0),
             reads=["modT"], writes=["modT"])
        T.barrier()


    def make_brow(dst, dst_name, col0):
        for half in range(2):
            p, pn = nextps()
            for cc in range(4):
                c = half * 4 + cc
                T.op("dve", lambda c=c: V.tensor_copy(out=bc_tmp[:], in_=modT[:, col0 + c:col0 + c + 1].to_broadcast([128, 128])),
                     reads=["modT"], writes=["bc_tmp"])
                T.op("pe", lambda cc=cc, p=p: PE.matmul(p[:, cc * 128:(cc + 1) * 128], lhsT=bc_tmp[:], rhs=identf[:],
                                                        start=True, stop=True),
                     reads=["bc_tmp", "identf"], writes=[pn])
            T.op("act", lambda half=half, p=p: A.copy(out=dst[:, half * 512:(half + 1) * 512], in_=p[:, 0:512]),
                 reads=[pn], writes=[dst_name])

    def ln_stats(y, yname):
        def f():
            V.bn_stats(out=stat[:, 0:6], in_=y[:, 0:512])
            return V.bn_stats(out=stat[:, 6:12], in_=y[:, 512:1024])
        T.op("dve", f, reads=[yname], writes=["st6"])
        T.op("dve", lambda: V.bn_aggr(out=stat[:, 12:14], in_=stat[:, 0:12]), reads=["st6"], writes=["mv"])

    def ln_apply(y, yname, g, b, gbname, out, outname):
        sd = stat[:, 14:15]
        rs = stat[:, 15:16]
        T.op("pool", lambda: P.tensor_scalar_add(out=sd, in0=stat[:, 13:14], scalar1=LN_EPS), reads=["mv"], writes=["sd"])
        T.op("pool", lambda: P.tensor_tensor(out=rs, in0=sd, in1=epsc[:, 0:1], op=ALU.pow), reads=["sd", "epsc"], writes=["rs"])
        T.op("dve", lambda: V.tensor_scalar(out=out[:], in0=y[:], scalar1=stat[:, 12:13], scalar2=rs,
                                            op0=ALU.subtract, op1=ALU.mult),
             reads=[yname, "mv", "rs"], writes=[outname])
        T.op("dve", lambda: V.tensor_tensor(out=out[:], in0=out[:], in1=g[:], op=ALU.mult),
             reads=[outname, gbname], writes=[outname])
        T.op("dve", lambda: V.tensor_tensor(out=out[:], in0=out[:], in1=b[:], op=ALU.add),
             reads=[outname, gbname], writes=[outname])

    def layer_norm(y, yname, g, b, gbname, out, outname, uid, ln_eng="pool"):
        ln_stats(y, yname)
        ln_apply(y, yname, g, b, gbname, out, outname)

    T.op("pool", lambda: P.memset(epsc[:], -0.5), writes=["epsc"])

    with ExitStack() as es:
        wspT = es.enter_context(nc.sbuf_tensor("wspT", [128, 8, 128], BF16))
        bsb = es.enter_context(nc.sbuf_tensor("bsb", [128, 1024], F32))
        ln1g = es.enter_context(nc.sbuf_tensor("ln1g", [128, 1024], F32))
        ln1b = es.enter_context(nc.sbuf_tensor("ln1b", [128, 1024], F32))
        lnvg = es.enter_context(nc.sbuf_tensor("lnvg", [128, 1024], F32))
        lnvb = es.enter_context(nc.sbuf_tensor("lnvb", [128, 1024], F32))
        mcur = es.enter_context(nc.sbuf_tensor("mcur", [128, 4, 128], BF16))
        mprev = es.enter_context(nc.sbuf_tensor("mprev", [128, 4, 128], BF16))
        esink = es.enter_context(nc.sbuf_tensor("esink", [128, 16], F32))
        xs2 = es.enter_context(nc.sbuf_tensor("xs", [128, 2, 1024], F32))
        hT = es.enter_context(nc.sbuf_tensor("hT", [128, 8, 128], BF16))
        uT = es.enter_context(nc.sbuf_tensor("uT", [128, 8, 128], BF16))
        vf = es.enter_context(nc.sbuf_tensor("vf", [128, 1024], F32))
        vn = es.enter_context(nc.sbuf_tensor("vn", [128, 1024], BF16))
        aT = es.enter_context(nc.sbuf_tensor("aT", [128, 8, 128], BF16))
        qT = es.enter_context(nc.sbuf_tensor("qT", [128, 8, 128], BF16))
        kTz = es.enter_context(nc.sbuf_tensor("kTz", [128, 2, 2, 2, 128], BF16))
        vext = es.enter_context(nc.sbuf_tensor("vext", [128, 2, 2, 128], BF16))
        prT = es.enter_context(nc.sbuf_tensor("prT", [128, 2, 8, 128], BF16))
        osb = es.enter_context(nc.sbuf_tensor("osb", [128, 16, 64], BF16))
        oT = es.enter_context(nc.sbuf_tensor("oT", [128, 8, 128], BF16))
        gT = es.enter_context(nc.sbuf_tensor("gT", [128, 16, 128], BF16))
        ysb = es.enter_context(nc.sbuf_tensor("ysb", [128, 1024], F32))
        rden = es.enter_context(nc.sbuf_tensor("rden", [128, 8], F32))
        mT = uT
        wspf = vf[:].rearrange("p (g t) -> p g t", g=8)
        T.op("sp", lambda: SP.dma_start(out=wspf, in_=wsp_d), writes=["vf"], chan="ld_wsp")
        bcast_row(bsb[:], "bsb", bsp_d, "ld_bsb")
        bcast_row(ln1g[:], "ln1gb", ln1g_d, "ld_ln1g")
        bcast_row(ln1b[:], "ln1gb", ln1b_d, "ld_ln1b")
        bcast_row(lnvg[:], "lnvgb", lnvg_d, "ld_lnvg")
        bcast_row(lnvb[:], "lnvgb", lnvb_d, "ld_lnvb")
        bcast_row(esink[:], "esink", sink_d, "ld_sink")
        T.op("act", lambda: A.activation(out=esink[:], in_=esink[:], func=AF.Exp), reads=["esink"], writes=["esink"])
        T.op("pool", lambda: P.affine_select(out=wspf, in_=wspf, pattern=[[0, 8], [1, 128]], compare_op=ALU.is_ge,
                                             fill=0.0, base=0, channel_multiplier=-1), reads=["vf"], writes=["vf"])
        T.op("dve", lambda: V.tensor_copy(out=wspT[:], in_=wspf), reads=["vf"], writes=["wspT"])
        T.op("pool", lambda: P.memset(mcur[:], 0.0), writes=["mcur"])
        T.op("pool", lambda: P.memset(mprev[:], 0.0), writes=["mprev"])
        T.op("pool", lambda: P.affine_select(out=mcur[:], in_=mcur[:], pattern=[[0, 4], [1, 128]], compare_op=ALU.is_ge,
                                             fill=NEG, base=0, channel_multiplier=-1), reads=["mcur"], writes=["mcur"])
        T.op("pool", lambda: P.affine_select(out=mprev[:], in_=mprev[:], pattern=[[0, 4], [-1, 128]], compare_op=ALU.is_gt,
                                             fill=NEG, base=0, channel_multiplier=1), reads=["mprev"], writes=["mprev"])
        T.op("pool", lambda: P.memset(kTz[:], 0.0), writes=["kTz0", "kTz1"])
        T.op("pool", lambda: P.memset(vext[:], 1.0), writes=["vext0", "vext1"])
        make_brow(ysb, "ysb", 16)
        for k in range(8):
            T.op("dve", lambda k=k: V.tensor_tensor(out=wout[:, k, :], in0=wout[:, k, :], in1=ysb[:], op=ALU.mult),
                 reads=["wout", "ysb"], writes=["wout"])

        NT1 = NT if phases != 'p2' else 0

        def load_x(i):
            T.op("sp", lambda: SP.dma_start(out=xs2[:, i % 2, :], in_=x_d[i * 128:(i + 1) * 128, :]), writes=["xs%d" % (i % 2)],
                 chan="ld_xs%d" % (i % 2))
        if NT1 > 0:
            load_x(0)
        CH = 512
        tb_chunks = list(range(0, NEXP, CH)) if phases != 'p1' else []
        tb_per_tile = (len(tb_chunks) + max(NT1, 1) - 1) // max(NT1, 1)

        def table_chunk(n_, r):
            T.op("pool", lambda: P.dma_start(out=tab_d[r:r + CH, 0:D], in_=pu_d[r:r + CH, :]), chan="tb%d" % (n_ % 4))
            T.op("pool", lambda: P.dma_start(out=tab_d[r:r + CH, D:2 * D], in_=pv_d[r:r + CH, :]), chan="tb%d" % (4 + n_ % 4))
        tb_done = [0]
        for i in range(NT1):
            par = i % 2
            xs, xsn = xs2[:, par, :], "xs%d" % par
            if i + 1 < NT1:
                load_x(i + 1)
            for _ in range(tb_per_tile):
                if tb_done[0] < len(tb_chunks):
                    table_chunk(tb_done[0], tb_chunks[tb_done[0]])
                    tb_done[0] += 1
            p, pn = nextps()

            def f(p=p, xs=xs):
                for c in range(8):
                    ins = PE.transpose(out=p[:, c * 128:(c + 1) * 128], in_=xs[:, c * 128:(c + 1) * 128], identity=identf[:])
                return ins
            T.op("pe", f, reads=[xsn, "identf"], writes=[pn])

            def f(p=p):
                for c in range(8):
                    ins = A.activation(out=hT[:, c, :], in_=p[:, c * 128:(c + 1) * 128], func=AF.Identity,
                                       bias=modT[:, c:c + 1], scale=modT[:, 8 + c:9 + c])
                return ins
            T.op("act", f, reads=[pn, "modT"], writes=["hT"])

            def fm_proj(w, col0, nchunks, rhs_tile):
                p, pn = nextps()

                def f(p=p):
                    for c in range(nchunks):
                        for k in range(8):
                            ins = PE.matmul(p[:, c * 128:(c + 1) * 128], lhsT=w[:, k, col0 + c * 128:col0 + (c + 1) * 128],
                                            rhs=rhs_tile[:, k, :], start=(k == 0), stop=(k == 7))
                    return ins
                return p, pn, f

            p, pn = nextps()

            def f(p=p):
                for half in range(2):
                    for k in range(8):
                        ins = PE.matmul(p[:, half * 512:(half + 1) * 512], lhsT=hT[:, k, :],
                                        rhs=win[:, k, 1024 + half * 512:1024 + (half + 1) * 512], start=(k == 0), stop=(k == 7))
                return ins
            T.op("pe", f, reads=["hT", "win"], writes=[pn])
            T.op("act", lambda p=p: A.activation(out=vf[:], in_=p[:], func=AF.Gelu), reads=[pn], writes=["vf"])
            p, pn, f = fm_proj(win, 0, 8, hT)
            T.op("pe", f, reads=["hT", "win"], writes=[pn])
            T.op("act", lambda p=p: A.activation(out=uT[:].rearrange("p c t -> p (c t)"), in_=p[:], func=AF.Gelu),
                 reads=[pn], writes=["uT"])
            layer_norm(vf, "vf", lnvg, lnvb, "lnvgb", vf, "vf", "v")
            T.op("dve", lambda: V.tensor_copy(out=vn[:], in_=vf[:]), reads=["vf"], writes=["vn"])
            p, pn, f = fm_proj(win, 2048, 8, hT)
            T.op("pe", f, reads=["hT", "win"], writes=[pn])
            T.op("act", lambda p=p: A.mul(out=qT[:].rearrange("p c t -> p (c t)"), in_=p[:], mul=0.125), reads=[pn], writes=["qT"])
            p, pn = nextps()

            def f(p=p):
                for j in range(2):
                    for k in range(8):
                        PE.matmul(p[:, j * 128:(j + 1) * 128], lhsT=wkd[:, k, j * 128:(j + 1) * 128], rhs=hT[:, k, :],
                                  start=(k == 0), stop=(k == 7))
                for k in range(8):
                    ins = PE.matmul(p[:, 512:640], lhsT=hT[:, k, :], rhs=win[:, k, 3200:3328], start=(k == 0), stop=(k == 7))
                return ins
            T.op("pe", f, reads=["hT", "win", "wkd"], writes=[pn])

            def f(p=p, par=par):
                for j in range(2):
                    A.copy(out=kTz[0:64, par, j, 0, :], in_=p[0:64, j * 128:(j + 1) * 128])
                    A.copy(out=kTz[64:128, par, j, 1, :], in_=p[64:128, j * 128:(j + 1) * 128])
                return A.copy(out=vext[:, par, :, 0:64], in_=p[:, 512:640].rearrange("p (j d) -> p j d", j=2))
            T.op("act", f, reads=[pn], writes=["kTz%d" % par, "vext%d" % par])
            p, pn = nextps()

            def f(p=p):
                for g in range(8):
                    ins = PE.matmul(p[:, g * 128:(g + 1) * 128], lhsT=vn[:, g * 128:(g + 1) * 128], rhs=wspT[:, g, :],
                                    start=True, stop=True)
                return ins
            T.op("pe", f, reads=["vn", "wspT"], writes=[pn])
            T.op("dve", lambda p=p: V.tensor_tensor(out=vf[:], in0=p[:], in1=bsb[:], op=ALU.add),
                 reads=[pn, "bsb"], writes=["vf"])
            T.op("dve", lambda: V.tensor_tensor(out=aT[:].rearrange("p c t -> p (c t)"), in0=vf[:],
                                                in1=uT[:].rearrange("p c t -> p (c t)"), op=ALU.mult),
                 reads=["vf", "uT"], writes=["aT"])
            for j in range(2):
                kbs = [(par, mcur, "mcur", 0)] + ([(1 - par, mprev, "mprev", 1)] if i > 0 else [])
                for (kp, msk, mname, slot) in kbs:
                    p, pn = nextps()

                    def f(p=p, kp=kp, msk=msk, j=j):
                        for half in range(2):
                            PE.matmul(p[:, half * 512:(half + 1) * 512], lhsT=identb[:], rhs=msk[:].rearrange("p a q -> p (a q)"),
                                      start=True, stop=False)
                            for hh4 in range(4):
                                hh = half * 4 + hh4
                                h = j * 8 + hh
                                ins = PE.matmul(p[:, hh * 128:(hh + 1) * 128], lhsT=kTz[:, kp, j, h % 2, :], rhs=qT[:, h // 2, :],
                                                start=False, stop=(hh4 == 3))
                        return ins
                    T.op("pe", f, reads=["identb", mname, "kTz%d" % kp, "qT"], writes=[pn])
                    T.op("act", lambda p=p, slot=slot: A.activation(out=prT[:, slot, :, :].rearrange("p h q -> p (h q)"), in_=p[:],
                                                                    func=AF.Exp), reads=[pn], writes=["prT%d" % slot])
                if j == 0:
                    for gh in range(2):
                        p, pn, f = fm_proj(win, 3328 + gh * 1024, 8, hT)
                        T.op("pe", f, reads=["hT", "win"], writes=[pn])
                        T.op("act", lambda p=p, gh=gh: A.activation(out=gT[:, gh * 8:(gh + 1) * 8, :].rearrange("p c t -> p (c t)"),
                                                                    in_=p[:], func=AF.Sigmoid), reads=[pn], writes=["gT"])
                    pA, pAn, f = fm_proj(wpa, 0, 8, aT)
                    T.op("pe", f, reads=["aT", "wpa"], writes=[pAn])
                    T.op("dve", lambda p=pA: V.tensor_tensor(out=vf[:], in0=p[:], in1=gT[:, 0:8, :].rearrange("p c t -> p (c t)"), op=ALU.mult),
                         reads=[pAn, "gT"], writes=["vf"])
                p, pn = nextps()

                def f(p=p, j=j, kbs=kbs):
                    for hh in range(8):
                        for n, (kp, msk, mname, slot) in enumerate(kbs):
                            ins = PE.matmul(p[:, hh * 128:(hh + 1) * 128], lhsT=prT[:, slot, hh, :], rhs=vext[:, kp, j, :],
                                            start=(n == 0), stop=(n == len(kbs) - 1))
                    return ins
                T.op("pe", f, reads=["prT0", "prT1", "vext0", "vext1"], writes=[pn])
                pv = p[:].rearrange("p (h d) -> p h d", h=8)
                T.op("dve", lambda pv=pv, j=j: V.tensor_tensor(out=rden[:], in0=pv[:, :, 64], in1=esink[:, j * 8:(j + 1) * 8], op=ALU.add),
                     reads=[pn, "esink"], writes=["rden"])
                T.op("dve", lambda: V.reciprocal(out=rden[:], in_=rden[:]), reads=["rden"], writes=["rden"])
                T.op("dve", lambda pv=pv, j=j: V.tensor_tensor(out=osb[:, j * 8:(j + 1) * 8, :], in0=pv[:, :, 0:64],
                                                               in1=rden[:].unsqueeze(2).to_broadcast([128, 8, 64]), op=ALU.mult),
                     reads=[pn, "rden"], writes=["osb"])
            p, pn = nextps()
            pb = p[:].bitcast(BF16)
            ofl = osb[:].rearrange("p h d -> p (h d)")

            def f(pb=pb, ofl=ofl):
                for c in range(8):
                    ins = PE.transpose(out=pb[:, c * 128:(c + 1) * 128], in_=ofl[:, c * 128:(c + 1) * 128], identity=identb[:])
                return ins
            T.op("pe", f, reads=["osb", "identb"], writes=[pn])
            T.op("act", lambda pb=pb: A.copy(out=oT[:].rearrange("p c t -> p (c t)"), in_=pb[:, 0:1024]), reads=[pn], writes=["oT"])
            p, pn, f = fm_proj(wpb, 0, 8, oT)
            T.op("pe", f, reads=["oT", "wpb"], writes=[pn])
            T.op("dve", lambda p=p: V.tensor_tensor(out=ysb[:], in0=p[:], in1=gT[:, 8:16, :].rearrange("p c t -> p (c t)"), op=ALU.mult),
                 reads=[pn, "gT"], writes=["ysb"])
            T.op("dve", lambda: V.tensor_tensor(out=mT[:].rearrange("p c t -> p (c t)"), in0=vf[:], in1=ysb[:], op=ALU.add),
                 reads=["vf", "ysb"], writes=["uT"])
            p, pn = nextps()

            def f(p=p):
                for half in range(2):
                    for k in range(8):
                        ins = PE.matmul(p[:, half * 512:(half + 1) * 512], lhsT=mT[:, k, :],
                                        rhs=wout[:, k, half * 512:(half + 1) * 512], start=(k == 0), stop=(k == 7))
                return ins
            T.op("pe", f, reads=["uT", "wout"], writes=[pn])
            T.op("dve", lambda p=p, xs=xs: V.scalar_tensor_tensor(out=ysb[:], in0=xs, scalar=ALPHA, in1=p[:], op0=ALU.mult, op1=ALU.add),
                 reads=[xsn, pn], writes=["ysb"])
            layer_norm(ysb, "ysb", ln1g, ln1b, "ln1gb", ysb, "ysb", "1")
            T.op("sp", lambda i=i: SP.dma_start(out=x1_d[i * 128:(i + 1) * 128, :], in_=ysb[:]), reads=["ysb"],
                 writes=["x1d%d" % i], chan="st_x1")
        while tb_done[0] < len(tb_chunks):
            table_chunk(tb_done[0], tb_chunks[tb_done[0]])
            tb_done[0] += 1
        T.barrier()
    esw.close()

    ps_n[0] = 3
    with ExitStack() as es:
        def S_(name, shape, dt):
            return es.enter_context(nc.sbuf_tensor(name, list(shape), dt))
        wpq = S_("wpq", [128, 8, 2048], BF16)
        keysT = S_("keysT", [128, 16, 128], BF16)
        sc2p = S_("sc2p", [128, 1024], F32)
        sh2 = S_("sh2", [128, 1024], F32)
        gt2p = S_("gt2p", [128, 1024], F32)
        ln2g = S_("ln2g", [128, 1024], F32)
        ln2b = S_("ln2b", [128, 1024], F32)
        x1s = S_("x1s", [128, 2, 1024], F32)
        h2 = S_("h2", [128, 2, 1024], F32)
        h2T = S_("h2T", [128, 8, 128], BF16)
        q2T = S_("q2T", [128, 16, 128], BF16)
        ssb = S_("ssb", [128, 16, 128], F32)
        s2a = S_("s2a", [128, 2, 128], F32)
        s2b = S_("s2b", [128, 2, 256], F32)
        v1 = S_("v1", [128, 16, 16], F32)
        i1 = S_("i1", [128, 16, 16], U32)
        i1f = S_("i1f", [128, 16, 16], BF16)
        cand = S_("cand", [128, 8, 256], F32)
        scv = S_("scv", [128, 8, 16], F32)
        ci = S_("ci", [128, 8, 16], U32)
        cia = S_("cia", [128, 8, 16], U32)
        cib = S_("cib", [128, 8, 16], U32)
        caf = S_("caf", [128, 8, 16], BF16)
        cbf = S_("cbf", [128, 8, 16], BF16)
        iota16 = S_("iota16", [128, 16], BF16)
        oh = S_("oh", [128, 8, 16, 16], BF16)
        sel1 = S_("sel1", [128, 8, 16], F32)
        sel2 = S_("sel2", [128, 8, 16], F32)
        ef = S_("ef", [128, 128], F32)
        ei = S_("ei", [128, 2, 128], I32)
        gsm = S_("gsm", [128, 2, 128], F32)
        gsum = S_("gsum", [128, 8], F32)
        actv = S_("actv", [128, 128], F32)
        gl = S_("gl", [128, 128], F32)
        wgt = S_("wgt", [128, 128], F32)
        uvb = S_("uvb", [128, NBU, 2048], BF16)
        vsc = S_("vsc", [128, 2, 1024], BF16)
        junk = S_("junk", [128, 1024], BF16)
        y2 = S_("y2", [128, 1024], F32)
        h2b = S_("h2b", [128, 2, 1024], BF16)
        diagb = S_("diagb", [128, 3, 128], BF16)

        for k in range(8):
            for n0 in (0, 1024):
                T.op("pool", lambda k=k, n0=n0: P.dma_start(out=wpq[:, k, n0:n0 + 1024], in_=wpq_d[k * 128:(k + 1) * 128, n0:n0 + 1024]),
                     writes=["wpq"], chan="ldw_wpq")
        T.op("pool", lambda: P.dma_start(out=keysT[:].rearrange("p j k -> p (j k)"), in_=keys_d.rearrange("p j k -> p (j k)")),
             writes=["keysT"], chan="ldw_keys")
        bcast_row(ln2g[:], "ln2gb", ln2g_d, "ld_ln2g")
        bcast_row(ln2b[:], "ln2gb", ln2b_d, "ld_ln2b")
        make_brow(sh2, "sh2", 24)
        make_brow(sc2p, "sc2p", 32)
        make_brow(gt2p, "gt2p", 40)
        T.op("pool", lambda: P.iota(iota16[:], pattern=[[1, 16]], base=0, channel_multiplier=0,
                                    allow_small_or_imprecise_dtypes=True), writes=["iota16"])
        PF, PFn = PS[3], "ps3"
        NT2 = NT if phases != 'p1' else 0

        def prep_ops(i):
            L = []
            par = i % 2
            X1, X1n = x1s[:, par, :], "x1s%d" % par
            H2, H2n = h2[:, par, :], "h2_%d" % par
            EI, EIn = ei[:, par, :], "ei%d" % par
            GS, GSn = gsm[:, par, :], "gsm%d" % par
            GS3 = GS.rearrange("p (h k) -> p h k", h=8)

            gap = [0]

            def Q(issuer, fn, reads=(), writes=(), chan=None):
                L.append((lambda: T.op(issuer, fn, reads, writes, chan), gap[0]))
                gap[0] = 0

            def G(n):
                gap[0] = n

            G(4)
            Q("sp", lambda: SP.dma_start(out=X1, in_=x1_d[i * 128:(i + 1) * 128, :]), reads=["x1d%d" % i], writes=[X1n], chan="ld_" + X1n)
            Q("dve", lambda: V.tensor_tensor(out=H2, in0=X1, in1=sc2p[:], op=ALU.mult), reads=[X1n, "sc2p"], writes=[H2n])
            Q("dve", lambda: V.tensor_tensor(out=H2, in0=H2, in1=sh2[:], op=ALU.add), reads=[H2n, "sh2"], writes=[H2n])
            Q("act", lambda: A.copy(out=h2b[:, par, :], in_=H2), reads=[H2n], writes=["h2b_%d" % par])
            G(2)
            p, pn = nextps(2)

            def f(p=p):
                for c in range(8):
                    ins = PE.transpose(out=p[:, c * 128:(c + 1) * 128], in_=H2[:, c * 128:(c + 1) * 128], identity=identf[:])
                return ins
            G(3)
            Q("pe", f, reads=[H2n, "identf"], writes=[pn])
            G(2)
            Q("act", lambda p=p: A.copy(out=h2T[:].rearrange("p c t -> p (c t)"), in_=p[:]), reads=[pn], writes=["h2T"])
            qps = []
            for qh in range(2):
                p, pn = nextps(2)
                qps.append((p, pn))
                for c0 in range(0, 8, 2):
                    def f(p=p, qh=qh, c0=c0):
                        for c in range(c0, c0 + 2):
                            j = qh * 8 + c
                            for k in range(8):
                                ins = PE.matmul(p[:, c * 128:(c + 1) * 128], lhsT=wpq[:, k, j * 128:(j + 1) * 128], rhs=h2T[:, k, :],
                                                start=(k == 0), stop=(k == 7))
                        return ins
                    G(4 if (qh == 1 and c0 == 6) else 1)
                    Q("pe", f, reads=["h2T", "wpq"], writes=[pn])
            for qh in range(2):
                p, pn = qps[qh]
                G(2 if qh == 1 else 0)
                Q("act", lambda p=p, qh=qh: A.copy(out=q2T[:, qh * 8:(qh + 1) * 8, :].rearrange("p c t -> p (c t)"), in_=p[:]),
                  reads=[pn], writes=["q2T%d" % qh])
            sps = []
            for qh in range(2):
                p, pn = nextps(2)
                sps.append((p, pn))

                def f(p=p, qh=qh):
                    for c in range(8):
                        j = qh * 8 + c
                        ins = PE.matmul(p[:, c * 128:(c + 1) * 128], lhsT=q2T[:, j, :], rhs=keysT[:, j, :], start=True, stop=True)
                    return ins
                G(3 if qh == 1 else 0)
                Q("pe", f, reads=["q2T%d" % qh, "keysT"], writes=[pn])
            for qh in range(2):
                p, pn = sps[qh]
                G(2 if qh == 1 else 0)
                Q("act", lambda p=p, qh=qh: A.copy(out=ssb[:, qh * 8:(qh + 1) * 8, :].rearrange("p c t -> p (c t)"), in_=p[:]),
                  reads=[pn], writes=["ssb%d" % qh])

            def top16(src, srcname, scratch, scrname, vals, inds, outname):
                Q("dve", lambda: V.max(out=vals[:, 0:8], in_=src), reads=[srcname], writes=[outname + "_v0"])
                Q("dve", lambda: V.max_index(out=inds[:, 0:8], in_max=vals[:, 0:8], in_values=src),
                  reads=[srcname, outname + "_v0"], writes=[outname + "_i"])
                Q("dve", lambda: V.match_replace(out=scratch, in_to_replace=vals[:, 0:8], in_values=src, imm_value=-1e30),
                  reads=[srcname, outname + "_v0"], writes=[scrname])
                Q("dve", lambda: V.max(out=vals[:, 8:16], in_=scratch), reads=[scrname], writes=[outname + "_v1"])
                Q("dve", lambda: V.max_index(out=inds[:, 8:16], in_max=vals[:, 8:16], in_values=scratch),
                  reads=[scrname, outname + "_v1"], writes=[outname + "_i"])

            for j in range(16):
                top16(ssb[:, j, :], "ssb%d" % (j // 8), s2a[:, j % 2, :], "s2a%d" % (j % 2), v1[:, j, :], i1[:, j, :], "t1_%d" % j)
            t1v = ["t1_%d_v0" % j for j in range(16)] + ["t1_%d_v1" % j for j in range(16)]
            t1i = ["t1_%d_i" % j for j in range(16)]
            Q("dve", lambda: V.tensor_copy(out=i1f[:], in_=i1[:]), reads=t1i, writes=["i1f"])
            v1v = v1[:].rearrange("p (h two) k -> p h two k", two=2)
            Q("dve", lambda: V.tensor_tensor(out=cand[:].rearrange("p h (a b) -> p h a b", a=16),
                                             in0=v1v[:, :, 0, :].unsqueeze(3).to_broadcast([128, 8, 16, 16]),
                                             in1=v1v[:, :, 1, :].unsqueeze(2).to_broadcast([128, 8, 16, 16]), op=ALU.add),
              reads=t1v, writes=["cand"])
            for h in range(8):
                top16(cand[:, h, :], "cand", s2b[:, h % 2, :], "s2b%d" % (h % 2), scv[:, h, :], ci[:, h, :], "t2_%d" % h)
            t2v = ["t2_%d_v0" % h for h in range(8)] + ["t2_%d_v1" % h for h in range(8)]
            t2i = ["t2_%d_i" % h for h in range(8)]
            Q("dve", lambda: V.tensor_single_scalar(out=cia[:], in_=ci[:], scalar=4, op=ALU.logical_shift_right), reads=t2i, writes=["cia"])
            Q("dve", lambda: V.tensor_single_scalar(out=cib[:], in_=ci[:], scalar=15, op=ALU.bitwise_and), reads=t2i, writes=["cib"])
            Q("dve", lambda: V.tensor_copy(out=caf[:], in_=cia[:]), reads=["cia"], writes=["caf"])
            Q("dve", lambda: V.tensor_copy(out=cbf[:], in_=cib[:]), reads=["cib"], writes=["cbf"])
            i1v = i1f[:].rearrange("p (h two) k -> p h two k", two=2)
            for (cf, cfn, half, sel, seln) in ((caf, "caf", 0, sel1, "sel1"), (cbf, "cbf", 1, sel2, "sel2")):
                Q("dve", lambda cf=cf: V.tensor_tensor(out=oh[:], in0=cf[:].unsqueeze(3).to_broadcast([128, 8, 16, 16]),
                                                       in1=iota16[:].unsqueeze(1).unsqueeze(1).to_broadcast([128, 8, 16, 16]),
                                                       op=ALU.is_equal), reads=[cfn, "iota16"], writes=["oh"])
                Q("dve", lambda half=half: V.tensor_tensor(out=oh[:], in0=oh[:],
                                                           in1=i1v[:, :, half, :].unsqueeze(2).to_broadcast([128, 8, 16, 16]),
                                                           op=ALU.mult), reads=["oh", "i1f"], writes=["oh"])
                Q("dve", lambda sel=sel: V.tensor_reduce(out=sel[:], in_=oh[:], axis=AX.X, op=ALU.add), reads=["oh"], writes=[seln])
            Q("dve", lambda: V.scalar_tensor_tensor(out=ef[:], in0=sel1[:].rearrange("p h k -> p (h k)"), scalar=128.0,
                                                    in1=sel2[:].rearrange("p h k -> p (h k)"), op0=ALU.mult, op1=ALU.add),
              reads=["sel1", "sel2"], writes=["ef"])
            Q("dve", lambda: V.tensor_copy(out=EI, in_=ef[:]), reads=["ef"], writes=[EIn])
            Q("dve", lambda: V.tensor_tensor(out=GS3, in0=scv[:], in1=scv[:, :, 0:1].to_broadcast([128, 8, 16]), op=ALU.subtract),
              reads=t2v, writes=[GSn])
            G(2)
            Q("act", lambda: A.activation(out=GS, in_=GS, func=AF.Exp), reads=[GSn], writes=[GSn])
            Q("dve", lambda: V.tensor_reduce(out=gsum[:], in_=GS3, axis=AX.X, op=ALU.add), reads=[GSn], writes=["gsum"])
            Q("dve", lambda: V.reciprocal(out=gsum[:], in_=gsum[:]), reads=["gsum"], writes=["gsum"])
            Q("dve", lambda: V.tensor_tensor(out=GS3, in0=GS3, in1=gsum[:].unsqueeze(2).to_broadcast([128, 8, 16]), op=ALU.mult),
              reads=[GSn, "gsum"], writes=[GSn])
            return L

        gcount = [0]
        PFs = [(PS[2], "ps2"), (PS[3], "ps3")]

        def slot_front(i, s):
            par = i % 2
            bi = gcount[0] % NBU
            gcount[0] += 1
            pi = s % 2
            T.op("pool", lambda: P.indirect_dma_start(
                out=uvb[:, bi, :], out_offset=None, in_=tab_d,
                in_offset=bass.IndirectOffsetOnAxis(ap=ei[:, par, s:s + 1], axis=0)),
                reads=["ei%d" % par], writes=["uvb%d" % bi], chan="g_uvb%d" % bi)
            T.op("dve", lambda: V.tensor_tensor(out=vsc[:, pi, :], in0=uvb[:, bi, 0:1024], in1=h2b[:, par, :], op=ALU.mult),
                 reads=["uvb%d" % bi, "h2b_%d" % par], writes=["prod%d" % pi])
            T.op("act", lambda: A.activation(out=junk[:], in_=vsc[:, pi, :], func=AF.Copy, accum_out=actv[:, s:s + 1]),
                 reads=["prod%d" % pi], writes=["junk", "actv%d" % s])
            T.op("act", lambda: A.activation(out=gl[:, s:s + 1], in_=actv[:, s:s + 1], func=AF.Gelu),
                 reads=["actv%d" % s], writes=["gl%d" % s])
            return bi

        def slot_back(i, s, bi):
            par = i % 2
            di = s % 3
            PF, PFn = PFs[i % 2]
            on_pool = POOL_DIAG_EVERY > 0 and s % POOL_DIAG_EVERY == POOL_DIAG_EVERY - 1
            E_, en_ = (P, "pool") if on_pool else (V, "dve")
            T.op(en_, lambda: E_.tensor_scalar(out=diagb[:, di, :], in0=identb[:], scalar1=gl[:, s:s + 1], scalar2=gsm[:, par, s:s + 1],
                                               op0=ALU.mult, op1=ALU.mult),
                 reads=["identb", "gl%d" % s, "gsm%d" % par], writes=["diag%d" % di])

            def f():
                for half in range(2):
                    ins = PE.matmul(PF[:, half * 512:(half + 1) * 512], lhsT=diagb[:, di, :],
                                    rhs=uvb[:, bi, 1024 + half * 512:1024 + (half + 1) * 512],
                                    start=(s == 0), stop=(s == 127))
                return ins
            T.op("pe", f, reads=["diag%d" % di, "uvb%d" % bi], writes=[PFn])

        def epi_a(i):
            par = i % 2
            PF, PFn = PFs[i % 2]
            T.op("dve", lambda: V.tensor_tensor(out=y2[:], in0=PF[:], in1=gt2p[:], op=ALU.mult), reads=[PFn, "gt2p"], writes=["y2"])
            T.op("dve", lambda: V.scalar_tensor_tensor(out=y2[:], in0=x1s[:, par, :], scalar=ALPHA, in1=y2[:], op0=ALU.mult, op1=ALU.add),
                 reads=["x1s%d" % par, "y2"], writes=["y2"])
            ln_stats(y2, "y2")

        def epi_b(i):
            ln_apply(y2, "y2", ln2g, ln2b, "ln2gb", y2, "y2")
            T.op("sp", lambda: SP.dma_start(out=out_d[i * 128:(i + 1) * 128, :], in_=y2[:]), reads=["y2"],
                 writes=["outd%d" % i], chan="st_out")

        EPI_A, EPI_B = 8, 12
        if NT2 > 0:
            for th, _g in prep_ops(0):
                th()
        pending = [(th, g_) for th, g_ in (prep_ops(1) if NT2 > 1 else [])]
        wait_slots = 0
        BLAG = 2
        for i in range(NT2):
            pend = []
            for s in range(128):
                bi = slot_front(i, s)
                pend.append((s, bi))
                if len(pend) > BLAG:
                    slot_back(i, *pend.pop(0))
                if i > 0 and s == EPI_A:
                    epi_a(i - 1)
                if i > 0 and s == EPI_B:
                    epi_b(i - 1)
                    pending = [(th, g_) for th, g_ in (prep_ops(i + 1) if i + 1 < NT2 else [])]
                    wait_slots = 0
                if wait_slots > 0:
                    wait_slots -= 1
                else:
                    for _ in range(TPS):
                        if not pending:
                            break
                        th, g_ = pending.pop(0)
                        th()
                        if g_ > 0:
                            wait_slots = g_
                            break
            while pend:
                slot_back(i, *pend.pop(0))
            while pending:
                pending.pop(0)[0]()
        if NT2 > 0:
            epi_a(NT2 - 1)
            epi_b(NT2 - 1)
        T.barrier()
    return nc


def make_in_maps(inputs, S=SEQ, cores=NCORES):
    f = lambda a: np.ascontiguousarray(np.asarray(a, dtype=np.float32))
    w_in = f(inputs["w_in"][0])
    k0 = w_in[:, 3072:3136]
    k1 = w_in[:, 3136:3200]
    shared = {
        "w_ada": f(inputs["w_ada"][0]),
        "b_ada_t": f(np.asarray(inputs["b_ada"][0]).reshape(48, 128).T),
        "w_in": w_in,
        "w_kdup": f(np.concatenate([k0, k0, k1, k1], axis=1)),
        "lnv_g": f(inputs["lnv_g"][0]), "lnv_b": f(inputs["lnv_b"][0]),
        "wsp_t": f(np.asarray(inputs["w_spatial"][0]).transpose(2, 0, 1)),
        "b_sp": f(np.asarray(inputs["b_spatial"][0]).reshape(-1)),
        "sinks": f(inputs["attn_sinks"][0]),
        "w_proj_a": f(inputs["w_proj_a"][0]), "w_proj_b": f(inputs["w_proj_b"][0]), "w_out": f(inputs["w_out"][0]),
        "ln1_g": f(inputs["ln1_g"][0]), "ln1_b": f(inputs["ln1_b"][0]),
        "w_pq": f(inputs["w_pq"][0]),
        "keys_t": f(np.stack([np.asarray(inputs["sub_keys1"][0]), np.asarray(inputs["sub_keys2"][0])], axis=1)
                    .reshape(16, 128, 128).transpose(2, 0, 1)),
        "peer_u": f(inputs["peer_u"][0]), "peer_v": f(inputs["peer_v"][0]),
        "ln2_g": f(inputs["ln2_g"][0]), "ln2_b": f(inputs["ln2_b"][0]),
    }
    x = np.asarray(inputs["x"], dtype=np.float32)
    c = np.asarray(inputs["c"], dtype=np.float32)
    maps = []
    for b in range(cores):
        m = dict(shared)
        m["x"] = np.ascontiguousarray(x[b, :S])
        m["c_t"] = np.ascontiguousarray(c[b].reshape(8, 128).T)
        maps.append(m)
    return maps


def kernel(**inputs):
    nc = build()
    maps = make_in_maps(inputs)
    res = run_bass_kernel_spmd(nc, maps, core_ids=list(range(NCORES)))
    return np.stack([np.asarray(r["out"], dtype=np.float32) for r in res.results], axis=0)
```

```python
import numpy as np
from contextlib import ExitStack
import concourse.bass as bass
import concourse.mybir as mybir
from concourse.bass_utils import run_bass_kernel_spmd

F32 = mybir.dt.float32
BF16 = mybir.dt.bfloat16
I32 = mybir.dt.int32
U32 = mybir.dt.uint32
AF = mybir.ActivationFunctionType
ALU = mybir.AluOpType
AX = mybir.AxisListType

D = 1024
SEQ = 8192
NCORES = 8
ALPHA = 2.0 ** 0.25
LN_EPS = 1e-5
NEXP = 16384
NEG = -30000.0


class Tracker:
    def __init__(self, nc):
        self.nc = nc
        self.eng = dict(pe=nc.tensor, act=nc.scalar, dve=nc.vector, pool=nc.gpsimd, sp=nc.sync)
        self.agents = {}
        for e in ("pe", "act", "dve", "pool"):
            self.agents[e] = [nc.alloc_semaphore("s_" + e), 0, 1]
        self.seen = {e: {} for e in self.eng}
        self.bufs = {}

    def _agent(self, name):
        if name not in self.agents:
            self.agents[name] = [self.nc.alloc_semaphore("c_" + name), 0, 16]
        return self.agents[name]

    def op(self, issuer, fn, reads=(), writes=(), chan=None):
        agent = chan if chan is not None else issuer
        ag = self._agent(agent)
        need = {}

        def add(a, t):
            if need.get(a, 0) < t:
                need[a] = t

        for b in reads:
            st = self.bufs.get(b)
            if st is not None and st[0] is not None:
                add(*st[0])
        for b in writes:
            st = self.bufs.get(b)
            if st is not None:
                if st[0] is not None and (st[0][0] != agent or chan is not None):
                    add(*st[0])
                for a, t in st[1].items():
                    if a != agent or chan is not None:
                        add(a, t)
        eng = self.eng[issuer]
        seen = self.seen[issuer]
        for a, t in need.items():
            if seen.get(a, 0) < t:
                sem, _, step = self.agents[a]
                eng.wait_ge(sem, t * step)
                seen[a] = t
        ins = fn()
        ag[1] += 1
        tick = ag[1]
        ins.then_inc(ag[0], ag[2])
        for b in reads:
            st = self.bufs.setdefault(b, [None, {}])
            st[1][agent] = tick
        for b in writes:
            self.bufs[b] = [(agent, tick), {}]
        return tick

    def barrier(self):
        for issuer, eng in self.eng.items():
            seen = self.seen[issuer]
            for a, (sem, cnt, step) in self.agents.items():
                if cnt > 0 and seen.get(a, 0) < cnt:
                    eng.wait_ge(sem, cnt * step)
                    seen[a] = cnt


def build(NT=SEQ // 128, phases='both', NBU=12, POOL_DIAG_EVERY=0, TPS=2):
    S = NT * 128
    nc = bass.Bass("TRN2", target_bir_lowering=False)
    T = Tracker(nc)

    def din(name, shape, dt=F32):
        return nc.dram_tensor(name, list(shape), dt, kind="ExternalInput").ap()

    x_d = din("x", [S, D])
    ct_d = din("c_t", [128, 8])
    wada_d = din("w_ada", [D, 6 * D])
    badaT_d = din("b_ada_t", [128, 48])
    win_d = din("w_in", [D, 5376])
    wkd_d = din("w_kdup", [D, 256])
    lnvg_d = din("lnv_g", [D])
    lnvb_d = din("lnv_b", [D])
    wsp_d = din("wsp_t", [128, 8, 128])
    bsp_d = din("b_sp", [D])
    sink_d = din("sinks", [16])
    wpa_d = din("w_proj_a", [D, D])
    wpb_d = din("w_proj_b", [D, D])
    wout_d = din("w_out", [D, D])
    ln1g_d = din("ln1_g", [D])
    ln1b_d = din("ln1_b", [D])
    wpq_d = din("w_pq", [D, 2048])
    keys_d = din("keys_t", [128, 16, 128])
    pu_d = din("peer_u", [NEXP, D])
    pv_d = din("peer_v", [NEXP, D])
    ln2g_d = din("ln2_g", [D])
    ln2b_d = din("ln2_b", [D])
    out_d = nc.dram_tensor("out", [S, D], F32, kind="ExternalOutput").ap()
    x1_d = nc.dram_tensor("x1_scratch", [S, D], F32, kind="Internal").ap()
    tab_d = nc.dram_tensor("uv_table", [NEXP, 2 * D], BF16, kind="Internal").ap()

    V, A, P, PE, SP = nc.vector, nc.scalar, nc.gpsimd, nc.tensor, nc.sync

    def sb(name, shape, dt):
        return nc.alloc_sbuf_tensor(name, list(shape), dt)

    PS = [nc.alloc_psum_tensor("ps%d" % i, [128, 1024], F32) for i in range(4)]
    ps_rr = [0]

    ps_n = [4]

    def nextps(n=None):
        n = n or ps_n[0]
        i = ps_rr[0] % n
        ps_rr[0] += 1
        return PS[i], "ps%d" % i

    identf = sb("identf", [128, 128], F32)
    identb = sb("identb", [128, 128], BF16)
    onesf = sb("onesf", [128, 128], F32)
    modT = sb("modT", [128, 48], F32)
    stat = sb("stat", [128, 64], F32)
    bc_tmp = sb("bc_tmp", [128, 128], F32)
    epsc = sb("epsc", [128, 1], F32)

    T.op("pool", lambda: P.memset(onesf[:], 1.0), writes=["onesf"])
    T.op("pool", lambda: P.affine_select(out=identf[:], in_=onesf[:], pattern=[[1, 128]], compare_op=ALU.is_equal,
                                         fill=0.0, base=0, channel_multiplier=-1), reads=["onesf"], writes=["identf"])
    T.op("dve", lambda: V.tensor_copy(out=identb[:], in_=identf[:]), reads=["identf"], writes=["identb"])

    def bcast_row(dst, dst_name, src_d, chan):
        T.op("sp", lambda: SP.dma_start(out=dst, in_=src_d.partition_broadcast(128)), writes=[dst_name], chan=chan)

    esw = ExitStack()
    win = esw.enter_context(nc.sbuf_tensor("win", [128, 8, 5376], BF16))
    wkd = esw.enter_context(nc.sbuf_tensor("wkd", [128, 8, 256], BF16))
    wpa = esw.enter_context(nc.sbuf_tensor("wpa", [128, 8, 1024], BF16))
    wpb = esw.enter_context(nc.sbuf_tensor("wpb", [128, 8, 1024], BF16))
    wout = esw.enter_context(nc.sbuf_tensor("wout", [128, 8, 1024], BF16))

    def load_w(dst, dname, src, ncols, step=1024):
        for k in range(8):
            for n0 in range(0, ncols, step):
                n1 = min(ncols, n0 + step)
                T.op("pool", lambda k=k, n0=n0, n1=n1: P.dma_start(out=dst[:, k, n0:n1], in_=src[k * 128:(k + 1) * 128, n0:n1]),
                     writes=[dname], chan="ldw_" + dname)
    if phases != 'p2':
        load_w(win, "win", win_d, 5376, step=1792)
        load_w(wkd, "wkd", wkd_d, 256)
        load_w(wpa, "wpa", wpa_d, 1024)
        load_w(wpb, "wpb", wpb_d, 1024)
        load_w(wout, "wout", wout_d, 1024)

    with ExitStack() as es:
        ct = es.enter_context(nc.sbuf_tensor("ct", [128, 8], F32))
        silc = es.enter_context(nc.sbuf_tensor("silc", [128, 8], F32))
        badaT = es.enter_context(nc.sbuf_tensor("badaT", [128, 48], F32))
        wa0 = es.enter_context(nc.sbuf_tensor("wa0", [128, 8, 256], F32))
        wa1 = es.enter_context(nc.sbuf_tensor("wa1", [128, 8, 256], F32))
        T.op("sp", lambda: SP.dma_start(out=ct[:], in_=ct_d), writes=["ct"], chan="ld_ct")
        T.op("sp", lambda: SP.dma_start(out=badaT[:], in_=badaT_d), writes=["badaT"], chan="ld_bada")
        T.op("act", lambda: A.activation(out=silc[:], in_=ct[:], func=AF.Silu), reads=["ct"], writes=["silc"])
        was = [wa0, wa1]
        pm, pmn = PS[3], "ps3"
        for nb in range(24):
            w = was[nb % 2]
            wn = "wa%d" % (nb % 2)
            T.op("sp", lambda w=w, nb=nb: SP.dma_start(
                out=w[:], in_=wada_d[:, nb * 256:(nb + 1) * 256].rearrange("(k p) n -> p k n", p=128)),
                writes=[wn], chan="ld_" + wn)

            def f(w=w, nb=nb):
                ins = None
                for jj in range(2):
                    j = nb * 2 + jj
                    for k in range(8):
                        ins = PE.matmul(pm[:, j:j + 1], lhsT=w[:, k, jj * 128:(jj + 1) * 128], rhs=silc[:, k:k + 1],
                                        start=(k == 0), stop=(k == 7))
                return ins
            T.op("pe", f, reads=[wn, "silc"], writes=[pmn])
        T.op("dve", lambda: V.tensor_tensor(out=modT[:], in0=pm[:, 0:48], in1=badaT[:], op=ALU.add),
             reads=[pmn, "badaT"], writes=["modT"])
        T.op("dve", lambda: V.tensor_scalar_add(out=modT[:, 8:24], in0=modT[:, 8:24], scalar1=1.0),
             reads=["modT"], writes=["modT"])
        T.op("dve", lambda: V.tensor_scalar_add(out=modT[:, 32:48], in0=modT[:, 32:48], scalar1=1.0),
             reads=["modT"], writes=["modT"])
        T.barrier()


    def make_brow(dst, dst_name, col0):
        for half in range(2):
            p, pn = nextps()
            for cc in range(4):
                c = half * 4 + cc
                T.op("dve", lambda c=c: V.tensor_copy(out=bc_tmp[:], in_=modT[:, col0 + c:col0 + c + 1].to_broadcast([128, 128])),
                     reads=["modT"], writes=["bc_tmp"])
                T.op("pe", lambda cc=cc, p=p: PE.matmul(p[:, cc * 128:(cc + 1) * 128], lhsT=bc_tmp[:], rhs=identf[:],
                                                        start=True, stop=True),
                     reads=["bc_tmp", "identf"], writes=[pn])
            T.op("act", lambda half=half, p=p: A.copy(out=dst[:, half * 512:(half + 1) * 512], in_=p[:, 0:512]),
                 reads=[pn], writes=[dst_name])

    def ln_stats(y, yname):
        def f():
            V.bn_stats(out=stat[:, 0:6], in_=y[:, 0:512])
            return V.bn_stats(out=stat[:, 6:12], in_=y[:, 512:1024])
        T.op("dve", f, reads=[yname], writes=["st6"])
        T.op("dve", lambda: V.bn_aggr(out=stat[:, 12:14], in_=stat[:, 0:12]), reads=["st6"], writes=["mv"])

    def ln_apply(y, yname, g, b, gbname, out, outname):
        sd = stat[:, 14:15]
        rs = stat[:, 15:16]
        T.op("pool", lambda: P.tensor_scalar_add(out=sd, in0=stat[:, 13:14], scalar1=LN_EPS), reads=["mv"], writes=["sd"])
        T.op("pool", lambda: P.tensor_tensor(out=rs, in0=sd, in1=epsc[:, 0:1], op=ALU.pow), reads=["sd", "epsc"], writes=["rs"])
        T.op("dve", lambda: V.tensor_scalar(out=out[:], in0=y[:], scalar1=stat[:, 12:13], scalar2=rs,
                                            op0=ALU.subtract, op1=ALU.mult),
             reads=[yname, "mv", "rs"], writes=[outname])
        T.op("dve", lambda: V.tensor_tensor(out=out[:], in0=out[:], in1=g[:], op=ALU.mult),
             reads=[outname, gbname], writes=[outname])
        T.op("dve", lambda: V.tensor_tensor(out=out[:], in0=out[:], in1=b[:], op=ALU.add),
             reads=[outname, gbname], writes=[outname])

    def layer_norm(y, yname, g, b, gbname, out, outname, uid, ln_eng="pool"):
        ln_stats(y, yname)
        ln_apply(y, yname, g, b, gbname, out, outname)

    T.op("pool", lambda: P.memset(epsc[:], -0.5), writes=["epsc"])

    with ExitStack() as es:
        wspT = es.enter_context(nc.sbuf_tensor("wspT", [128, 8, 128], BF16))
        bsb = es.enter_context(nc.sbuf_tensor("bsb", [128, 1024], F32))
        ln1g = es.enter_context(nc.sbuf_tensor("ln1g", [128, 1024], F32))
        ln1b = es.enter_context(nc.sbuf_tensor("ln1b", [128, 1024], F32))
        lnvg = es.enter_context(nc.sbuf_tensor("lnvg", [128, 1024], F32))
        lnvb = es.enter_context(nc.sbuf_tensor("lnvb", [128, 1024], F32))
        mcur = es.enter_context(nc.sbuf_tensor("mcur", [128, 4, 128], BF16))
        mprev = es.enter_context(nc.sbuf_tensor("mprev", [128, 4, 128], BF16))
        esink = es.enter_context(nc.sbuf_tensor("esink", [128, 16], F32))
        xs2 = es.enter_context(nc.sbuf_tensor("xs", [128, 2, 1024], F32))
        hT = es.enter_context(nc.sbuf_tensor("hT", [128, 8, 128], BF16))
        uT = es.enter_context(nc.sbuf_tensor("uT", [128, 8, 128], BF16))
        vf = es.enter_context(nc.sbuf_tensor("vf", [128, 1024], F32))
        vn = es.enter_context(nc.sbuf_tensor("vn", [128, 1024], BF16))
        aT = es.enter_context(nc.sbuf_tensor("aT", [128, 8, 128], BF16))
        qT = es.enter_context(nc.sbuf_tensor("qT", [128, 8, 128], BF16))
        kTz = es.enter_context(nc.sbuf_tensor("kTz", [128, 2, 2, 2, 128], BF16))
        vext = es.enter_context(nc.sbuf_tensor("vext", [128, 2, 2, 128], BF16))
        prT = es.enter_context(nc.sbuf_tensor("prT", [128, 2, 8, 128], BF16))
        osb = es.enter_context(nc.sbuf_tensor("osb", [128, 16, 64], BF16))
        oT = es.enter_context(nc.sbuf_tensor("oT", [128, 8, 128], BF16))
        gT = es.enter_context(nc.sbuf_tensor("gT", [128, 16, 128], BF16))
        ysb = es.enter_context(nc.sbuf_tensor("ysb", [128, 1024], F32))
        rden = es.enter_context(nc.sbuf_tensor("rden", [128, 8], F32))
        mT = uT
        wspf = vf[:].rearrange("p (g t) -> p g t", g=8)
        T.op("sp", lambda: SP.dma_start(out=wspf, in_=wsp_d), writes=["vf"], chan="ld_wsp")
        bcast_row(bsb[:], "bsb", bsp_d, "ld_bsb")
        bcast_row(ln1g[:], "ln1gb", ln1g_d, "ld_ln1g")
        bcast_row(ln1b[:], "ln1gb", ln1b_d, "ld_ln1b")
        bcast_row(lnvg[:], "lnvgb", lnvg_d, "ld_lnvg")
        bcast_row(lnvb[:], "lnvgb", lnvb_d, "ld_lnvb")
        bcast_row(esink[:], "esink", sink_d, "ld_sink")
        T.op("act", lambda: A.activation(out=esink[:], in_=esink[:], func=AF.Exp), reads=["esink"], writes=["esink"])
        T.op("pool", lambda: P.affine_select(out=wspf, in_=wspf, pattern=[[0, 8], [1, 128]], compare_op=ALU.is_ge,
                                             fill=0.0, base=0, channel_multiplier=-1), reads=["vf"], writes=["vf"])
        T.op("dve", lambda: V.tensor_copy(out=wspT[:], in_=wspf), reads=["vf"], writes=["wspT"])
        T.op("pool", lambda: P.memset(mcur[:], 0.0), writes=["mcur"])
        T.op("pool", lambda: P.memset(mprev[:], 0.0), writes=["mprev"])
        T.op("pool", lambda: P.affine_select(out=mcur[:], in_=mcur[:], pattern=[[0, 4], [1, 128]], compare_op=ALU.is_ge,
                                             fill=NEG, base=0, channel_multiplier=-1), reads=["mcur"], writes=["mcur"])
        T.op("pool", lambda: P.affine_select(out=mprev[:], in_=mprev[:], pattern=[[0, 4], [-1, 128]], compare_op=ALU.is_gt,
                                             fill=NEG, base=0, channel_multiplier=1), reads=["mprev"], writes=["mprev"])
        T.op("pool", lambda: P.memset(kTz[:], 0.0), writes=["kTz0", "kTz1"])
        T.op("pool", lambda: P.memset(vext[:], 1.0), writes=["vext0", "vext1"])
        make_brow(ysb, "ysb", 16)
        for k in range(8):
            T.op("dve", lambda k=k: V.tensor_tensor(out=wout[:, k, :], in0=wout[:, k, :], in1=ysb[:], op=ALU.mult),
                 reads=["wout", "ysb"], writes=["wout"])

        NT1 = NT if phases != 'p2' else 0

        def load_x(i):
            T.op("sp", lambda: SP.dma_start(out=xs2[:, i % 2, :], in_=x_d[i * 128:(i + 1) * 128, :]), writes=["xs%d" % (i % 2)],
                 chan="ld_xs%d" % (i % 2))
        if NT1 > 0:
            load_x(0)
        CH = 512
        tb_chunks = list(range(0, NEXP, CH)) if phases != 'p1' else []
        tb_per_tile = (len(tb_chunks) + max(NT1, 1) - 1) // max(NT1, 1)

        def table_chunk(n_, r):
            T.op("pool", lambda: P.dma_start(out=tab_d[r:r + CH, 0:D], in_=pu_d[r:r + CH, :]), chan="tb%d" % (n_ % 4))
            T.op("pool", lambda: P.dma_start(out=tab_d[r:r + CH, D:2 * D], in_=pv_d[r:r + CH, :]), chan="tb%d" % (4 + n_ % 4))
        tb_done = [0]
        for i in range(NT1):
            par = i % 2
            xs, xsn = xs2[:, par, :], "xs%d" % par
            if i + 1 < NT1:
                load_x(i + 1)
            for _ in range(tb_per_tile):
                if tb_done[0] < len(tb_chunks):
                    table_chunk(tb_done[0], tb_chunks[tb_done[0]])
                    tb_done[0] += 1
            p, pn = nextps()

            def f(p=p, xs=xs):
                for c in range(8):
                    ins = PE.transpose(out=p[:, c * 128:(c + 1) * 128], in_=xs[:, c * 128:(c + 1) * 128], identity=identf[:])
                return ins
            T.op("pe", f, reads=[xsn, "identf"], writes=[pn])

            def f(p=p):
                for c in range(8):
                    ins = A.activation(out=hT[:, c, :], in_=p[:, c * 128:(c + 1) * 128], func=AF.Identity,
                                       bias=modT[:, c:c + 1], scale=modT[:, 8 + c:9 + c])
                return ins
            T.op("act", f, reads=[pn, "modT"], writes=["hT"])

            def fm_proj(w, col0, nchunks, rhs_tile):
                p, pn = nextps()

                def f(p=p):
                    for c in range(nchunks):
                        for k in range(8):
                            ins = PE.matmul(p[:, c * 128:(c + 1) * 128], lhsT=w[:, k, col0 + c * 128:col0 + (c + 1) * 128],
                                            rhs=rhs_tile[:, k, :], start=(k == 0), stop=(k == 7))
                    return ins
                return p, pn, f

            p, pn = nextps()

            def f(p=p):
                for half in range(2):
                    for k in range(8):
                        ins = PE.matmul(p[:, half * 512:(half + 1) * 512], lhsT=hT[:, k, :],
                                        rhs=win[:, k, 1024 + half * 512:1024 + (half + 1) * 512], start=(k == 0), stop=(k == 7))
                return ins
            T.op("pe", f, reads=["hT", "win"], writes=[pn])
            T.op("act", lambda p=p: A.activation(out=vf[:], in_=p[:], func=AF.Gelu), reads=[pn], writes=["vf"])
            p, pn, f = fm_proj(win, 0, 8, hT)
            T.op("pe", f, reads=["hT", "win"], writes=[pn])
            T.op("act", lambda p=p: A.activation(out=uT[:].rearrange("p c t -> p (c t)"), in_=p[:], func=AF.Gelu),
                 reads=[pn], writes=["uT"])
            layer_norm(vf, "vf", lnvg, lnvb, "lnvgb", vf, "vf", "v")
            T.op("dve", lambda: V.tensor_copy(out=vn[:], in_=vf[:]), reads=["vf"], writes=["vn"])
            p, pn, f = fm_proj(win, 2048, 8, hT)
            T.op("pe", f, reads=["hT", "win"], writes=[pn])
            T.op("act", lambda p=p: A.mul(out=qT[:].rearrange("p c t -> p (c t)"), in_=p[:], mul=0.125), reads=[pn], writes=["qT"])
            p, pn = nextps()

            def f(p=p):
                for j in range(2):
                    for k in range(8):
                        PE.matmul(p[:, j * 128:(j + 1) * 128], lhsT=wkd[:, k, j * 128:(j + 1) * 128], rhs=hT[:, k, :],
                                  start=(k == 0), stop=(k == 7))
                for k in range(8):
                    ins = PE.matmul(p[:, 512:640], lhsT=hT[:, k, :], rhs=win[:, k, 3200:3328], start=(k == 0), stop=(k == 7))
                return ins
            T.op("pe", f, reads=["hT", "win", "wkd"], writes=[pn])

            def f(p=p, par=par):
                for j in range(2):
                    A.copy(out=kTz[0:64, par, j, 0, :], in_=p[0:64, j * 128:(j + 1) * 128])
                    A.copy(out=kTz[64:128, par, j, 1, :], in_=p[64:128, j * 128:(j + 1) * 128])
                return A.copy(out=vext[:, par, :, 0:64], in_=p[:, 512:640].rearrange("p (j d) -> p j d", j=2))
            T.op("act", f, reads=[pn], writes=["kTz%d" % par, "vext%d" % par])
            p, pn = nextps()

            def f(p=p):
                for g in range(8):
                    ins = PE.matmul(p[:, g * 128:(g + 1) * 128], lhsT=vn[:, g * 128:(g + 1) * 128], rhs=wspT[:, g, :],
                                    start=True, stop=True)
                return ins
            T.op("pe", f, reads=["vn", "wspT"], writes=[pn])
            T.op("dve", lambda p=p: V.tensor_tensor(out=vf[:], in0=p[:], in1=bsb[:], op=ALU.add),
                 reads=[pn, "bsb"], writes=["vf"])
            T.op("dve", lambda: V.tensor_tensor(out=aT[:].rearrange("p c t -> p (c t)"), in0=vf[:],
                                                in1=uT[:].rearrange("p c t -> p (c t)"), op=ALU.mult),
                 reads=["vf", "uT"], writes=["aT"])
            for j in range(2):
                kbs = [(par, mcur, "mcur", 0)] + ([(1 - par, mprev, "mprev", 1)] if i > 0 else [])
                for (kp, msk, mname, slot) in kbs:
                    p, pn = nextps()

                    def f(p=p, kp=kp, msk=msk, j=j):
                        for half in range(2):
                            PE.matmul(p[:, half * 512:(half + 1) * 512], lhsT=identb[:], rhs=msk[:].rearrange("p a q -> p (a q)"),
                                      start=True, stop=False)
                            for hh4 in range(4):
                                hh = half * 4 + hh4
                                h = j * 8 + hh
                                ins = PE.matmul(p[:, hh * 128:(hh + 1) * 128], lhsT=kTz[:, kp, j, h % 2, :], rhs=qT[:, h // 2, :],
                                                start=False, stop=(hh4 == 3))
                        return ins
                    T.op("pe", f, reads=["identb", mname, "kTz%d" % kp, "qT"], writes=[pn])
                    T.op("act", lambda p=p, slot=slot: A.activation(out=prT[:, slot, :, :].rearrange("p h q -> p (h q)"), in_=p[:],
                                                                    func=AF.Exp), reads=[pn], writes=["prT%d" % slot])
                if j == 0:
                    for gh in range(2):
                        p, pn, f = fm_proj(win, 3328 + gh * 1024, 8, hT)
                        T.op("pe", f, reads=["hT", "win"], writes=[pn])
                        T.op("act", lambda p=p, gh=gh: A.activation(out=gT[:, gh * 8:(gh + 1) * 8, :].rearrange("p c t -> p (c t)"),
                                                                    in_=p[:], func=AF.Sigmoid), reads=[pn], writes=["gT"])
                    pA, pAn, f = fm_proj(wpa, 0, 8, aT)
                    T.op("pe", f, reads=["aT", "wpa"], writes=[pAn])
                    T.op("dve", lambda p=pA: V.tensor_tensor(out=vf[:], in0=p[:], in1=gT[:, 0:8, :].rearrange("p c t -> p (c t)"), op=ALU.mult),
                         reads=[pAn, "gT"], writes=["vf"])
                p, pn = nextps()

                def f(p=p, j=j, kbs=kbs):
                    for hh in range(8):
                        for n, (kp, msk, mname, slot) in enumerate(kbs):
                            ins = PE.matmul(p[:, hh * 128:(hh + 1) * 128], lhsT=prT[:, slot, hh, :], rhs=vext[:, kp, j, :],
                                            start=(n == 0), stop=(n == len(kbs) - 1))
                    return ins
                T.op("pe", f, reads=["prT0", "prT1", "vext0", "vext1"], writes=[pn])
                pv = p[:].rearrange("p (h d) -> p h d", h=8)
                T.op("dve", lambda pv=pv, j=j: V.tensor_tensor(out=rden[:], in0=pv[:, :, 64], in1=esink[:, j * 8:(j + 1) * 8], op=ALU.add),
                     reads=[pn, "esink"], writes=["rden"])
                T.op("dve", lambda: V.reciprocal(out=rden[:], in_=rden[:]), reads=["rden"], writes=["rden"])
                T.op("dve", lambda pv=pv, j=j: V.tensor_tensor(out=osb[:, j * 8:(j + 1) * 8, :], in0=pv[:, :, 0:64],
                                                               in1=rden[:].unsqueeze(2).to_broadcast([128, 8, 64]), op=ALU.mult),
                     reads=[pn, "rden"], writes=["osb"])
            p, pn = nextps()
            pb = p[:].bitcast(BF16)
            ofl = osb[:].rearrange("p h d -> p (h d)")

            def f(pb=pb, ofl=ofl):
                for c in range(8):
                    ins = PE.transpose(out=pb[:, c * 128:(c + 1) * 128], in_=ofl[:, c * 128:(c + 1) * 128], identity=identb[:])
                return ins
            T.op("pe", f, reads=["osb", "identb"], writes=[pn])
            T.op("act", lambda pb=pb: A.copy(out=oT[:].rearrange("p c t -> p (c t)"), in_=pb[:, 0:1024]), reads=[pn], writes=["oT"])
            p, pn, f = fm_proj(wpb, 0, 8, oT)
            T.op("pe", f, reads=["oT", "wpb"], writes=[pn])
            T.op("dve", lambda p=p: V.tensor_tensor(out=ysb[:], in0=p[:], in1=gT[:, 8:16, :].rearrange("p c t -> p (c t)"), op=ALU.mult),
                 reads=[pn, "gT"], writes=["ysb"])
            T.op("dve", lambda: V.tensor_tensor(out=mT[:].rearrange("p c t -> p (c t)"), in0=vf[:], in1=ysb[:], op=ALU.add),
                 reads=["vf", "ysb"], writes=["uT"])
            p, pn = nextps()

            def f(p=p):
                for half in range(2):
                    for k in range(8):
                        ins = PE.matmul(p[:, half * 512:(half + 1) * 512], lhsT=mT[:, k, :],
                                        rhs=wout[:, k, half * 512:(half + 1) * 512], start=(k == 0), stop=(k == 7))
                return ins
            T.op("pe", f, reads=["uT", "wout"], writes=[pn])
            T.op("dve", lambda p=p, xs=xs: V.scalar_tensor_tensor(out=ysb[:], in0=xs, scalar=ALPHA, in1=p[:], op0=ALU.mult, op1=ALU.add),
                 reads=[xsn, pn], writes=["ysb"])
            layer_norm(ysb, "ysb", ln1g, ln1b, "ln1gb", ysb, "ysb", "1")
            T.op("sp", lambda i=i: SP.dma_start(out=x1_d[i * 128:(i + 1) * 128, :], in_=ysb[:]), reads=["ysb"],
                 writes=["x1d%d" % i], chan="st_x1")
        while tb_done[0] < len(tb_chunks):
            table_chunk(tb_done[0], tb_chunks[tb_done[0]])
            tb_done[0] += 1
        T.barrier()
    esw.close()

    ps_n[0] = 3
    with ExitStack() as es:
        def S_(name, shape, dt):
            return es.enter_context(nc.sbuf_tensor(name, list(shape), dt))
        wpq = S_("wpq", [128, 8, 2048], BF16)
        keysT = S_("keysT", [128, 16, 128], BF16)
        sc2p = S_("sc2p", [128, 1024], F32)
        sh2 = S_("sh2", [128, 1024], F32)
        gt2p = S_("gt2p", [128, 1024], F32)
        ln2g = S_("ln2g", [128, 1024], F32)
        ln2b = S_("ln2b", [128, 1024], F32)
        x1s = S_("x1s", [128, 2, 1024], F32)
        h2 = S_("h2", [128, 2, 1024], F32)
        h2T = S_("h2T", [128, 8, 128], BF16)
        q2T = S_("q2T", [128, 16, 128], BF16)
        ssb = S_("ssb", [128, 16, 128], F32)
        s2a = S_("s2a", [128, 2, 128], F32)
        s2b = S_("s2b", [128, 2, 256], F32)
        v1 = S_("v1", [128, 16, 16], F32)
        i1 = S_("i1", [128, 16, 16], U32)
        i1f = S_("i1f", [128, 16, 16], BF16)
        cand = S_("cand", [128, 8, 256], F32)
        scv = S_("scv", [128, 8, 16], F32)
        ci = S_("ci", [128, 8, 16], U32)
        cia = S_("cia", [128, 8, 16], U32)
        cib = S_("cib", [128, 8, 16], U32)
        caf = S_("caf", [128, 8, 16], BF16)
        cbf = S_("cbf", [128, 8, 16], BF16)
        iota16 = S_("iota16", [128, 16], BF16)
        oh = S_("oh", [128, 8, 16, 16], BF16)
        sel1 = S_("sel1", [128, 8, 16], F32)
        sel2 = S_("sel2", [128, 8, 16], F32)
        ef = S_("ef", [128, 128], F32)
        ei = S_("ei", [128, 2, 128], I32)
        gsm = S_("gsm", [128, 2, 128], F32)
        gsum = S_("gsum", [128, 8], F32)
        actv = S_("actv", [128, 128], F32)
        gl = S_("gl", [128, 128], F32)
        wgt = S_("wgt", [128, 128], F32)
        uvb = S_("uvb", [128, NBU, 2048], BF16)
        vsc = S_("vsc", [128, 2, 1024], BF16)
        junk = S_("junk", [128, 1024], BF16)
        y2 = S_("y2", [128, 1024], F32)
        h2b = S_("h2b", [128, 2, 1024], BF16)
        diagb = S_("diagb", [128, 3, 128], BF16)

        for k in range(8):
            for n0 in (0, 1024):
                T.op("pool", lambda k=k, n0=n0: P.dma_start(out=wpq[:, k, n0:n0 + 1024], in_=wpq_d[k * 128:(k + 1) * 128, n0:n0 + 1024]),
                     writes=["wpq"], chan="ldw_wpq")
        T.op("pool", lambda: P.dma_start(out=keysT[:].rearrange("p j k -> p (j k)"), in_=keys_d.rearrange("p j k -> p (j k)")),
             writes=["keysT"], chan="ldw_keys")
        bcast_row(ln2g[:], "ln2gb", ln2g_d, "ld_ln2g")
        bcast_row(ln2b[:], "ln2gb", ln2b_d, "ld_ln2b")
        make_brow(sh2, "sh2", 24)
        make_brow(sc2p, "sc2p", 32)
        make_brow(gt2p, "gt2p", 40)
        T.op("pool", lambda: P.iota(iota16[:], pattern=[[1, 16]], base=0, channel_multiplier=0,
                                    allow_small_or_imprecise_dtypes=True), writes=["iota16"])
        PF, PFn = PS[3], "ps3"
        NT2 = NT if phases != 'p1' else 0

        def prep_ops(i):
            L = []
            par = i % 2
            X1, X1n = x1s[:, par, :], "x1s%d" % par
            H2, H2n = h2[:, par, :], "h2_%d" % par
            EI, EIn = ei[:, par, :], "ei%d" % par
            GS, GSn = gsm[:, par, :], "gsm%d" % par
            GS3 = GS.rearrange("p (h k) -> p h k", h=8)

            gap = [0]

            def Q(issuer, fn, reads=(), writes=(), chan=None):
                L.append((lambda: T.op(issuer, fn, reads, writes, chan), gap[0]))
                gap[0] = 0

            def G(n):
                gap[0] = n

            G(4)
            Q("sp", lambda: SP.dma_start(out=X1, in_=x1_d[i * 128:(i + 1) * 128, :]), reads=["x1d%d" % i], writes=[X1n], chan="ld_" + X1n)
            Q("dve", lambda: V.tensor_tensor(out=H2, in0=X1, in1=sc2p[:], op=ALU.mult), reads=[X1n, "sc2p"], writes=[H2n])
            Q("dve", lambda: V.tensor_tensor(out=H2, in0=H2, in1=sh2[:], op=ALU.add), reads=[H2n, "sh2"], writes=[H2n])
            Q("act", lambda: A.copy(out=h2b[:, par, :], in_=H2), reads=[H2n], writes=["h2b_%d" % par])
            G(2)
            p, pn = nextps(2)

            def f(p=p):
                for c in range(8):
                    ins = PE.transpose(out=p[:, c * 128:(c + 1) * 128], in_=H2[:, c * 128:(c + 1) * 128], identity=identf[:])
                return ins
            G(3)
            Q("pe", f, reads=[H2n, "identf"], writes=[pn])
            G(2)
            Q("dve", lambda p=p: V.tensor_copy(out=h2T[:].rearrange("p c t -> p (c t)"), in_=p[:]), reads=[pn], writes=["h2T"])
            qps = []
            for qh in range(2):
                p, pn = nextps(2)
                qps.append((p, pn))
                for c0 in range(0, 8, 2):
                    def f(p=p, qh=qh, c0=c0):
                        for c in range(c0, c0 + 2):
                            j = qh * 8 + c
                            for k in range(8):
                                ins = PE.matmul(p[:, c * 128:(c + 1) * 128], lhsT=wpq[:, k, j * 128:(j + 1) * 128], rhs=h2T[:, k, :],
                                                start=(k == 0), stop=(k == 7))
                        return ins
                    G(4 if (qh == 1 and c0 == 6) else 1)
                    Q("pe", f, reads=["h2T", "wpq"], writes=[pn])
            for qh in range(2):
                p, pn = qps[qh]
                G(2 if qh == 1 else 0)
                Q("dve", lambda p=p, qh=qh: V.tensor_copy(out=q2T[:, qh * 8:(qh + 1) * 8, :].rearrange("p c t -> p (c t)"), in_=p[:]),
                  reads=[pn], writes=["q2T%d" % qh])
            sps = []
            for qh in range(2):
                p, pn = nextps(2)
                sps.append((p, pn))

                def f(p=p, qh=qh):
                    for c in range(8):
                        j = qh * 8 + c
                        ins = PE.matmul(p[:, c * 128:(c + 1) * 128], lhsT=q2T[:, j, :], rhs=keysT[:, j, :], start=True, stop=True)
                    return ins
                G(3 if qh == 1 else 0)
                Q("pe", f, reads=["q2T%d" % qh, "keysT"], writes=[pn])
            for qh in range(2):
                p, pn = sps[qh]
                G(2 if qh == 1 else 0)
                Q("dve", lambda p=p, qh=qh: V.tensor_copy(out=ssb[:, qh * 8:(qh + 1) * 8, :].rearrange("p c t -> p (c t)"), in_=p[:]),
                  reads=[pn], writes=["ssb%d" % qh])

            def top16(src, srcname, scratch, scrname, vals, inds, outname):
                Q("dve", lambda: V.max(out=vals[:, 0:8], in_=src), reads=[srcname], writes=[outname + "_v0"])
                Q("dve", lambda: V.max_index(out=inds[:, 0:8], in_max=vals[:, 0:8], in_values=src),
                  reads=[srcname, outname + "_v0"], writes=[outname + "_i"])
                Q("dve", lambda: V.match_replace(out=scratch, in_to_replace=vals[:, 0:8], in_values=src, imm_value=-1e30),
                  reads=[srcname, outname + "_v0"], writes=[scrname])
                Q("dve", lambda: V.max(out=vals[:, 8:16], in_=scratch), reads=[scrname], writes=[outname + "_v1"])
                Q("dve", lambda: V.max_index(out=inds[:, 8:16], in_max=vals[:, 8:16], in_values=scratch),
                  reads=[scrname, outname + "_v1"], writes=[outname + "_i"])

            for j in range(16):
                top16(ssb[:, j, :], "ssb%d" % (j // 8), s2a[:, j % 2, :], "s2a%d" % (j % 2), v1[:, j, :], i1[:, j, :], "t1_%d" % j)
            t1v = ["t1_%d_v0" % j for j in range(16)] + ["t1_%d_v1" % j for j in range(16)]
            t1i = ["t1_%d_i" % j for j in range(16)]
            Q("dve", lambda: V.tensor_copy(out=i1f[:], in_=i1[:]), reads=t1i, writes=["i1f"])
            v1v = v1[:].rearrange("p (h two) k -> p h two k", two=2)
            Q("dve", lambda: V.tensor_tensor(out=cand[:].rearrange("p h (a b) -> p h a b", a=16),
                                             in0=v1v[:, :, 0, :].unsqueeze(3).to_broadcast([128, 8, 16, 16]),
                                             in1=v1v[:, :, 1, :].unsqueeze(2).to_broadcast([128, 8, 16, 16]), op=ALU.add),
              reads=t1v, writes=["cand"])
            for h in range(8):
                top16(cand[:, h, :], "cand", s2b[:, h % 2, :], "s2b%d" % (h % 2), scv[:, h, :], ci[:, h, :], "t2_%d" % h)
            t2v = ["t2_%d_v0" % h for h in range(8)] + ["t2_%d_v1" % h for h in range(8)]
            t2i = ["t2_%d_i" % h for h in range(8)]
            Q("dve", lambda: V.tensor_single_scalar(out=cia[:], in_=ci[:], scalar=4, op=ALU.logical_shift_right), reads=t2i, writes=["cia"])
            Q("dve", lambda: V.tensor_single_scalar(out=cib[:], in_=ci[:], scalar=15, op=ALU.bitwise_and), reads=t2i, writes=["cib"])
            Q("dve", lambda: V.tensor_copy(out=caf[:], in_=cia[:]), reads=["cia"], writes=["caf"])
            Q("dve", lambda: V.tensor_copy(out=cbf[:], in_=cib[:]), reads=["cib"], writes=["cbf"])
            i1v = i1f[:].rearrange("p (h two) k -> p h two k", two=2)
            for (cf, cfn, half, sel, seln) in ((caf, "caf", 0, sel1, "sel1"), (cbf, "cbf", 1, sel2, "sel2")):
                Q("dve", lambda cf=cf: V.tensor_tensor(out=oh[:], in0=cf[:].unsqueeze(3).to_broadcast([128, 8, 16, 16]),
                                                       in1=iota16[:].unsqueeze(1).unsqueeze(1).to_broadcast([128, 8, 16, 16]),
                                                       op=ALU.is_equal), reads=[cfn, "iota16"], writes=["oh"])
                Q("dve", lambda half=half: V.tensor_tensor(out=oh[:], in0=oh[:],
                                                           in1=i1v[:, :, half, :].unsqueeze(2).to_broadcast([128, 8, 16, 16]),
                                                           op=ALU.mult), reads=["oh", "i1f"], writes=["oh"])
                Q("dve", lambda sel=sel: V.tensor_reduce(out=sel[:], in_=oh[:], axis=AX.X, op=ALU.add), reads=["oh"], writes=[seln])
            Q("dve", lambda: V.scalar_tensor_tensor(out=ef[:], in0=sel1[:].rearrange("p h k -> p (h k)"), scalar=128.0,
                                                    in1=sel2[:].rearrange("p h k -> p (h k)"), op0=ALU.mult, op1=ALU.add),
              reads=["sel1", "sel2"], writes=["ef"])
            Q("dve", lambda: V.tensor_copy(out=EI, in_=ef[:]), reads=["ef"], writes=[EIn])
            Q("dve", lambda: V.tensor_tensor(out=GS3, in0=scv[:], in1=scv[:, :, 0:1].to_broadcast([128, 8, 16]), op=ALU.subtract),
              reads=t2v, writes=[GSn])
            G(2)
            Q("act", lambda: A.activation(out=GS, in_=GS, func=AF.Exp), reads=[GSn], writes=[GSn])
            Q("dve", lambda: V.tensor_reduce(out=gsum[:], in_=GS3, axis=AX.X, op=ALU.add), reads=[GSn], writes=["gsum"])
            Q("dve", lambda: V.reciprocal(out=gsum[:], in_=gsum[:]), reads=["gsum"], writes=["gsum"])
            Q("dve", lambda: V.tensor_tensor(out=GS3, in0=GS3, in1=gsum[:].unsqueeze(2).to_broadcast([128, 8, 16]), op=ALU.mult),
              reads=[GSn, "gsum"], writes=[GSn])
            return L

        gcount = [0]
        PFs = [(PS[2], "ps2"), (PS[3], "ps3")]

        def slot_front(i, s):
            par = i % 2
            bi = gcount[0] % NBU
            gcount[0] += 1
            pi = s % 2
            T.op("pool", lambda: P.indirect_dma_start(
                out=uvb[:, bi, :], out_offset=None, in_=tab_d,
                in_offset=bass.IndirectOffsetOnAxis(ap=ei[:, par, s:s + 1], axis=0)),
                reads=["ei%d" % par], writes=["uvb%d" % bi], chan="g_uvb%d" % bi)
            T.op("dve", lambda: V.tensor_tensor(out=vsc[:, pi, :], in0=uvb[:, bi, 0:1024], in1=h2b[:, par, :], op=ALU.mult),
                 reads=["uvb%d" % bi, "h2b_%d" % par], writes=["prod%d" % pi])
            T.op("act", lambda: A.activation(out=junk[:], in_=vsc[:, pi, :], func=AF.Copy, accum_out=actv[:, s:s + 1]),
                 reads=["prod%d" % pi], writes=["junk", "actv%d" % s])
            T.op("act", lambda: A.activation(out=gl[:, s:s + 1], in_=actv[:, s:s + 1], func=AF.Gelu),
                 reads=["actv%d" % s], writes=["gl%d" % s])
            return bi

        def slot_back(i, s, bi):
            par = i % 2
            di = s % 3
            PF, PFn = PFs[i % 2]
            on_pool = POOL_DIAG_EVERY > 0 and s % POOL_DIAG_EVERY == POOL_DIAG_EVERY - 1
            E_, en_ = (P, "pool") if on_pool else (V, "dve")
            T.op(en_, lambda: E_.tensor_scalar(out=diagb[:, di, :], in0=identb[:], scalar1=gl[:, s:s + 1], scalar2=gsm[:, par, s:s + 1],
                                               op0=ALU.mult, op1=ALU.mult),
                 reads=["identb", "gl%d" % s, "gsm%d" % par], writes=["diag%d" % di])

            def f():
                for half in range(2):
                    ins = PE.matmul(PF[:, half * 512:(half + 1) * 512], lhsT=diagb[:, di, :],
                                    rhs=uvb[:, bi, 1024 + half * 512:1024 + (half + 1) * 512],
                                    start=(s == 0), stop=(s == 127))
                return ins
            T.op("pe", f, reads=["diag%d" % di, "uvb%d" % bi], writes=[PFn])

        def epi_a(i):
            par = i % 2
            PF, PFn = PFs[i % 2]
            T.op("dve", lambda: V.tensor_tensor(out=y2[:], in0=PF[:], in1=gt2p[:], op=ALU.mult), reads=[PFn, "gt2p"], writes=["y2"])
            T.op("dve", lambda: V.scalar_tensor_tensor(out=y2[:], in0=x1s[:, par, :], scalar=ALPHA, in1=y2[:], op0=ALU.mult, op1=ALU.add),
                 reads=["x1s%d" % par, "y2"], writes=["y2"])
            ln_stats(y2, "y2")

        def epi_b(i):
            ln_apply(y2, "y2", ln2g, ln2b, "ln2gb", y2, "y2")
            T.op("sp", lambda: SP.dma_start(out=out_d[i * 128:(i + 1) * 128, :], in_=y2[:]), reads=["y2"],
                 writes=["outd%d" % i], chan="st_out")

        EPI_A, EPI_B = 8, 12
        if NT2 > 0:
            for th, _g in prep_ops(0):
                th()
        pending = [(th, g_) for th, g_ in (prep_ops(1) if NT2 > 1 else [])]
        wait_slots = 0
        BLAG = 2
        for i in range(NT2):
            pend = []
            for s in range(128):
                bi = slot_front(i, s)
                pend.append((s, bi))
                if len(pend) > BLAG:
                    slot_back(i, *pend.pop(0))
                if i > 0 and s == EPI_A:
                    epi_a(i - 1)
                if i > 0 and s == EPI_B:
                    epi_b(i - 1)
                    pending = [(th, g_) for th, g_ in (prep_ops(i + 1) if i + 1 < NT2 else [])]
                    wait_slots = 0
                if wait_slots > 0:
                    wait_slots -= 1
                else:
                    for _ in range(TPS):
                        if not pending:
                            break
                        th, g_ = pending.pop(0)
                        th()
                        if g_ > 0:
                            wait_slots = g_
                            break
            while pend:
                slot_back(i, *pend.pop(0))
            while pending:
                pending.pop(0)[0]()
        if NT2 > 0:
            epi_a(NT2 - 1)
            epi_b(NT2 - 1)
        T.barrier()
    return nc


def make_in_maps(inputs, S=SEQ, cores=NCORES):
    f = lambda a: np.ascontiguousarray(np.asarray(a, dtype=np.float32))
    w_in = f(inputs["w_in"][0])
    k0 = w_in[:, 3072:3136]
    k1 = w_in[:, 3136:3200]
    shared = {
        "w_ada": f(inputs["w_ada"][0]),
        "b_ada_t": f(np.asarray(inputs["b_ada"][0]).reshape(48, 128).T),
        "w_in": w_in,
        "w_kdup": f(np.concatenate([k0, k0, k1, k1], axis=1)),
        "lnv_g": f(inputs["lnv_g"][0]), "lnv_b": f(inputs["lnv_b"][0]),
        "wsp_t": f(np.asarray(inputs["w_spatial"][0]).transpose(2, 0, 1)),
        "b_sp": f(np.asarray(inputs["b_spatial"][0]).reshape(-1)),
        "sinks": f(inputs["attn_sinks"][0]),
        "w_proj_a": f(inputs["w_proj_a"][0]), "w_proj_b": f(inputs["w_proj_b"][0]), "w_out": f(inputs["w_out"][0]),
        "ln1_g": f(inputs["ln1_g"][0]), "ln1_b": f(inputs["ln1_b"][0]),
        "w_pq": f(inputs["w_pq"][0]),
        "keys_t": f(np.stack([np.asarray(inputs["sub_keys1"][0]), np.asarray(inputs["sub_keys2"][0])], axis=1)
                    .reshape(16, 128, 128).transpose(2, 0, 1)),
        "peer_u": f(inputs["peer_u"][0]), "peer_v": f(inputs["peer_v"][0]),
        "ln2_g": f(inputs["ln2_g"][0]), "ln2_b": f(inputs["ln2_b"][0]),
    }
    x = np.asarray(inputs["x"], dtype=np.float32)
    c = np.asarray(inputs["c"], dtype=np.float32)
    maps = []
    for b in range(cores):
        m = dict(shared)
        m["x"] = np.ascontiguousarray(x[b, :S])
        m["c_t"] = np.ascontiguousarray(c[b].reshape(8, 128).T)
        maps.append(m)
    return maps


def kernel(**inputs):
    nc = build()
    maps = make_in_maps(inputs)
    res = run_bass_kernel_spmd(nc, maps, core_ids=list(range(NCORES)))
    return np.stack([np.asarray(r["out"], dtype=np.float32) for r in res.results], axis=0)
```
